# Optimizing a Trainium2 kernel written in Bass

```python
import math
import jax, jax.numpy as jnp
from jax import lax
import numpy as np

D_MODEL = 1024
BATCH = 8
SEQ = 4096
DEPTH = 4

N_A_LAYERS = DEPTH // 2
N_B_LAYERS = DEPTH - N_A_LAYERS

CHUNK = 128
A_WIDTH = 2 * D_MODEL
A_GROUPS = 8
A_GROUP_DIM = A_WIDTH // A_GROUPS

N_HEADS = 16
N_KV_HEADS = 4
Q_PER_KV = N_HEADS // N_KV_HEADS
HEAD_DIM = 64
WINDOW = 128
BLOCK = 128

N_BUCKETS = 32
MAX_DISTANCE = 128

D_FF = 4 * D_MODEL
EPS = 1e-6

kernel_name = "yoco_sgu_swa_sink_hybrid"


def rms_norm(x, g):
    xf = x.astype(jnp.float32)
    y = xf * lax.rsqrt(jnp.mean(xf * xf, axis=-1, keepdims=True) + EPS)
    return (y * g.astype(jnp.float32)).astype(x.dtype)


def layer_norm(x, g):
    xf = x.astype(jnp.float32)
    mu = jnp.mean(xf, axis=-1, keepdims=True)
    xc = xf - mu
    y = xc * lax.rsqrt(jnp.mean(xc * xc, axis=-1, keepdims=True) + EPS)
    return (y * g.astype(jnp.float32)).astype(x.dtype)


def t5_causal_bucket(dist):
    max_exact = N_BUCKETS // 2
    d = jnp.maximum(dist, 0)
    log_ratio = jnp.log(jnp.maximum(d, 1).astype(jnp.float32) / max_exact) / math.log(MAX_DISTANCE / max_exact)
    large = jnp.minimum(max_exact + (log_ratio * (N_BUCKETS - max_exact)).astype(jnp.int32), N_BUCKETS - 1)
    return jnp.where(d < max_exact, d, large)


def band_bias_and_mask(rel_bias, n_blocks):
    qi = jnp.arange(BLOCK)[:, None]
    kj = jnp.arange(2 * BLOCK)[None, :]
    dist = qi + BLOCK - kj
    in_window = (dist >= 0) & (dist < WINDOW)
    bias = jnp.transpose(rel_bias.astype(jnp.float32)[t5_causal_bucket(dist)], (2, 0, 1))
    k_pos = (jnp.arange(n_blocks)[:, None, None] - 1) * BLOCK + kj[None]
    mask = in_window[None] & (k_pos >= 0)
    return bias, mask


def chunked_sgu_mixer(h, w_in, ln_g, w_s, b_s, w_out):
    B, S, _ = h.shape
    uv = jax.nn.gelu(h @ w_in, approximate=False)
    u, v = jnp.split(uv, 2, axis=-1)
    v = layer_norm(v, ln_g).reshape(B, S // CHUNK, CHUNK, A_GROUPS, A_GROUP_DIM)
    causal = jnp.tril(jnp.ones((CHUNK, CHUNK), dtype=bool))
    w_causal = jnp.where(causal[None], w_s, jnp.zeros_like(w_s))
    mixed = jnp.einsum('gij,bnjgd->bnigd', w_causal, v) + jnp.transpose(b_s)[None, None, :, :, None]
    gated = u * mixed.reshape(B, S, A_WIDTH)
    return gated @ w_out


def shared_band_kv(h, kv_norm_g, w_k, w_v):
    B, S, _ = h.shape
    nb = S // BLOCK
    hn = rms_norm(h, kv_norm_g)
    k = (hn @ w_k).reshape(B, nb, BLOCK, N_KV_HEADS, HEAD_DIM)
    v = (hn @ w_v).reshape(B, nb, BLOCK, N_KV_HEADS, HEAD_DIM)

    def band(t):
        prev = jnp.pad(t, ((0, 0), (1, 0), (0, 0), (0, 0), (0, 0)))[:, :-1]
        return jnp.concatenate([prev, t], axis=2)

    return band(k), band(v)


def sliding_sink_attention(h, k_band, v_band, w_q, sinks, w_o, band_bias, band_mask):
    B, S, _ = h.shape
    nb = S // BLOCK
    q = (h @ w_q).reshape(B, nb, BLOCK, N_KV_HEADS, Q_PER_KV, HEAD_DIM)
    logits = jnp.einsum('bnqkgd,bnjkd->bnkgqj', q, k_band).astype(jnp.float32) * (HEAD_DIM ** -0.5)
    logits = logits + band_bias.reshape(N_KV_HEADS, Q_PER_KV, BLOCK, 2 * BLOCK)
    logits = jnp.where(band_mask[None, :, None, None], logits, jnp.finfo(jnp.float32).min)
    sink = sinks.astype(jnp.float32).reshape(N_KV_HEADS, Q_PER_KV)[None, None, :, :, None, None]
    m = jnp.maximum(jnp.max(logits, axis=-1, keepdims=True), sink)
    e = jnp.exp(logits - m)
    probs = e / (jnp.sum(e, axis=-1, keepdims=True) + jnp.exp(sink - m))
    out = jnp.einsum('bnkgqj,bnjkd->bnqkgd', probs.astype(v_band.dtype), v_band)
    return out.reshape(B, S, N_HEADS * HEAD_DIM) @ w_o


def sq_relu_mlp(h, w1, w2):
    return jnp.square(jax.nn.relu(h @ w1)) @ w2


def setup_inputs(seed: int = 0) -> dict:
    key = jax.random.key(seed)
    ks = jax.random.split(key, 20)
    f32 = jnp.float32

    def nrm(k, shape, scale):
        return jax.random.normal(k, shape, f32) * scale

    return {
        "x": jax.random.normal(ks[0], (BATCH, SEQ, D_MODEL), f32),
        "mix_norm_g": 1.0 + nrm(ks[1], (DEPTH, D_MODEL), 0.05),
        "ffn_norm_g": 1.0 + nrm(ks[2], (DEPTH, D_MODEL), 0.05),
        "a_w_in": nrm(ks[3], (N_A_LAYERS, D_MODEL, 2 * A_WIDTH), D_MODEL ** -0.5),
        "a_ln_g": 1.0 + nrm(ks[4], (N_A_LAYERS, A_WIDTH), 0.05),
        "a_w_spatial": nrm(ks[5], (N_A_LAYERS, A_GROUPS, CHUNK, CHUNK), CHUNK ** -0.5),
        "a_b_spatial": 1.0 + nrm(ks[6], (N_A_LAYERS, A_GROUPS, CHUNK), 0.1),
        "a_w_out": nrm(ks[7], (N_A_LAYERS, A_WIDTH, D_MODEL), A_WIDTH ** -0.5),
        "kv_norm_g": 1.0 + nrm(ks[8], (D_MODEL,), 0.05),
        "w_k": nrm(ks[9], (D_MODEL, N_KV_HEADS * HEAD_DIM), D_MODEL ** -0.5),
        "w_v": nrm(ks[10], (D_MODEL, N_KV_HEADS * HEAD_DIM), D_MODEL ** -0.5),
        "b_w_q": nrm(ks[11], (N_B_LAYERS, D_MODEL, N_HEADS * HEAD_DIM), D_MODEL ** -0.5),
        "b_sinks": nrm(ks[12], (N_B_LAYERS, N_HEADS), 0.5),
        "b_w_o": nrm(ks[13], (N_B_LAYERS, N_HEADS * HEAD_DIM, D_MODEL), (N_HEADS * HEAD_DIM) ** -0.5),
        "rel_bias": nrm(ks[14], (N_BUCKETS, N_HEADS), 0.5),
        "ffn_w1": nrm(ks[15], (DEPTH, D_MODEL, D_FF), D_MODEL ** -0.5),
        "ffn_w2": nrm(ks[16], (DEPTH, D_FF, D_MODEL), 0.5 * D_FF ** -0.5),
        "final_norm_g": 1.0 + nrm(ks[17], (D_MODEL,), 0.05),
    }


def reference(x, mix_norm_g, ffn_norm_g, a_w_in, a_ln_g, a_w_spatial, a_b_spatial, a_w_out,
              kv_norm_g, w_k, w_v, b_w_q, b_sinks, b_w_o, rel_bias, ffn_w1, ffn_w2, final_norm_g):
    _, S, _ = x.shape
    band_bias, band_mask = band_bias_and_mask(rel_bias, S // BLOCK)
    h = x
    k_band = None
    v_band = None
    for layer in range(DEPTH):
        hn = rms_norm(h, mix_norm_g[layer])
        if layer < N_A_LAYERS:
            i = layer
            h = h + chunked_sgu_mixer(hn, a_w_in[i], a_ln_g[i], a_w_spatial[i], a_b_spatial[i], a_w_out[i])
        else:
            if layer == N_A_LAYERS:
                k_band, v_band = shared_band_kv(h, kv_norm_g, w_k, w_v)
                hn = rms_norm(h, mix_norm_g[layer])
            j = layer - N_A_LAYERS
            h = h + sliding_sink_attention(hn, k_band, v_band, b_w_q[j], b_sinks[j], b_w_o[j], band_bias, band_mask)
        h = h + sq_relu_mlp(rms_norm(h, ffn_norm_g[layer]), ffn_w1[layer], ffn_w2[layer])
    return rms_norm(h, final_norm_g)
```

```python
import math
import os
from contextlib import ExitStack

import numpy as np
import concourse.bass as bass
import concourse.mybir as mybir
from concourse.bass_utils import run_bass_kernel_spmd

F32 = mybir.dt.float32
BF16 = mybir.dt.bfloat16
AF = mybir.ActivationFunctionType
ALU = mybir.AluOpType

SEQ = 4096
D = 1024
TT = 1024
NSUB = TT // 512
NCH = TT // 128
NSLOT = 6
EPS = 1e-6
OPT_HNSPLIT = int(os.environ.get("K_HNSPLIT", "0"))
OPT_LNEXP = int(os.environ.get("K_LNEXP", "1"))
OPT_DEPTH = int(os.environ.get("K_DEPTH", "2"))
OPT_LNHALF = int(os.environ.get("K_LNHALF", "0"))


class Op:
    __slots__ = ("eng", "fn", "dma", "deps", "signal", "done_key", "done_val")


class Prog:
    ENG = ("pe", "act", "dve", "pool", "sp")

    def __init__(self):
        self.eng_ops = {e: [] for e in self.ENG}
        self.res = {}
        self.dma_cnt = {}
        self.deferred = []

    def add(self, eng, fn, reads=(), writes=(), dma=None):
        o = Op()
        o.eng, o.fn, o.dma, o.signal = eng, fn, dma, False
        if dma is not None:
            c = self.dma_cnt.get(dma, 0) + 1
            self.dma_cnt[dma] = c
            o.done_key, o.done_val = dma, 16 * c
        else:
            o.done_key, o.done_val = eng, None
        deps = {}

        def need(d, raw, waw=False, strict=False):
            if d is o:
                return
            if waw and dma is not None and d.dma == dma:
                return
            if d.dma is None and d.eng == eng and dma is None:
                if eng == "pe" or not (raw or strict):
                    return
            deps[id(d)] = d

        res = self.res
        for r in reads:
            st = res.get(r)
            if st is None:
                st = res[r] = [None, {}]
            if st[0] is not None:
                need(st[0], True)
        for w in writes:
            st = res.get(w)
            if st is None:
                st = res[w] = [None, {}]
            strict = (w[0] == "c")
            if st[0] is not None:
                need(st[0], False, True, strict)
            for rd in st[1].values():
                need(rd, False, False, strict)
        for r in reads:
            res[r][1][o.done_key] = o
        for w in writes:
            st = res[w]
            st[0] = o
            st[1] = {}
        o.deps = list(deps.values())
        for d in o.deps:
            d.signal = True
        self.eng_ops[eng].append(o)
        return o

    def defer(self, n, fn):
        self.deferred.append([n, fn])

    def tick(self):
        due = [d for d in self.deferred if d[0] <= 1]
        self.deferred = [[d[0] - 1, d[1]] for d in self.deferred if d[0] > 1]
        for d in due:
            d[1]()

    def flush(self):
        due, self.deferred = self.deferred, []
        for d in due:
            d[1]()

    def finalize(self):
        for e in self.ENG:
            c = 0
            for o in self.eng_ops[e]:
                if o.dma is None and o.signal:
                    c += 1
                    o.done_val = c

    def emit(self, ename, eng, sems, final_waits=()):
        waited = {}
        for o in self.eng_ops[ename]:
            for d in o.deps:
                k, v = d.done_key, d.done_val
                if waited.get(k, 0) < v:
                    eng.wait_ge(sems[k], v)
                    waited[k] = v
            ins = o.fn(eng)
            if o.dma is not None:
                ins.then_inc(sems[o.dma], 16)
            elif o.signal:
                ins.then_inc(sems[ename], 1)
        for k in final_waits:
            v = 16 * self.dma_cnt.get(k, 0)
            if v:
                eng.wait_ge(sems[k], v)


def _bucket_table():
    qi = np.arange(128)[:, None]
    kj = np.arange(256)[None, :]
    dist = qi + 128 - kj
    d = np.maximum(dist, 0)
    lr = (np.log(np.maximum(d, 1).astype(np.float32) / np.float32(16)) /
          np.float32(math.log(128 / 16))).astype(np.float32)
    large = np.minimum(16 + (lr * np.float32(16)).astype(np.int32), 31)
    bucket = np.where(d < 16, d, large)
    inwin = (dist >= 0) & (dist < 128)
    return bucket, inwin


def build_nc(n_tiles=4, layers=(0, 1, 2, 3), final_norm=True, debug=None):
    nc = bass.Bass("TRN2", target_bir_lowering=False)
    P = Prog()

    def din(name, shape):
        return nc.dram_tensor(name, list(shape), F32, kind="ExternalInput").ap()

    x_d = din("x", (SEQ, D))
    gstack_d = din("gstack", (80, 128))
    a_w_in_d = din("a_w_in", (2, 1024, 4096))
    a_ln_g_d = din("a_ln_g", (2, 2048))
    a_w_sp_d = din("a_w_spatial", (2, 8, 128, 128))
    a_b_sp_d = din("a_b_spatial", (2, 8, 128))
    a_w_out_d = din("a_w_out", (2, 2048, 1024))
    w_k_d = din("w_k", (1024, 256))
    w_v_d = din("w_v", (1024, 256))
    b_w_q_d = din("b_w_q", (2, 1024, 1024))
    b_sinks_d = din("b_sinks", (2, 16))
    b_w_o_d = din("b_w_o", (2, 1024, 1024))
    biasg_d = din("biasg", (128, 16 * 256))
    ffn_w1_d = din("ffn_w1", (4, 1024, 4096))
    ffn_w2_d = din("ffn_w2", (4, 4096, 1024))
    ident_d = din("ident", (128, 128))
    utri_d = din("utri", (128, 128))
    amask_d = din("amask", (128, 256))
    y_d = nc.dram_tensor("y", [SEQ, D], F32, kind="ExternalOutput").ap()

    def sb(name, shape, dt):
        return nc.alloc_sbuf_tensor(name, list(shape), dt)

    XT = sb("XT", [128, 8, TT], F32)
    HN = sb("HN", [128, 8, TT], BF16)
    BIG = sb("BIG", [128, 16, TT], BF16)
    RING = sb("RING", [128, NSLOT, 4096], BF16)
    S = sb("S", [128, 6144], F32)
    KTC = sb("KTC", [128, 4, TT], BF16)
    VC = sb("VC", [128, 8, 768], BF16)
    KTP = sb("KTP", [128, 4, 128], BF16)
    VP = sb("VP", [128, 768], BF16)
    EXPB = sb("EXPB", [128, 8, 512], F32)
    RB = sb("RB", [128, 2, 8, 128], BF16)
    WST = sb("WST", [128, 2, 8, 128], BF16)
    IDENT = sb("IDENT", [128, 128], F32)
    UTRI = sb("UTRI", [128, 128], F32)
    ONES = sb("ONES", [128, 128], BF16)
    ONESK = sb("ONESK", [128, 128], BF16)
    E2 = sb("E2", [128, 128], BF16)
    GT = sb("GT", [128, 80], F32)
    ES = sb("ES", [128, 2, 16], F32)
    ESC = sb("ESC", [128, 2, 4, 2], F32)
    HONES = sb("HONES", [128, 192], BF16)
    STAT = sb("STAT", [128, 2, 4, 6], F32)
    MV = sb("MV", [128, 2, 4], F32)
    GST = sb("GST", [80, 128], F32)

    LNG = KTC[:, :, :].rearrange("p a b -> p (a b)").bitcast(F32)

    PS = [nc.alloc_psum_tensor("ps%d" % i, [128, 512], F32) for i in range(8)]
    ps_rr = [0]

    def bank():
        i = ps_rr[0]
        ps_rr[0] = (i + 1) % 8
        return PS[i], ("ps", i)

    def sview(lo, hi):
        return S[:, lo:hi], [("S", b) for b in range(lo // 256, (hi - 1) // 256 + 1)]

    def xin(i):
        return sview(i * 1024, (i + 1) * 1024)

    def v_t(i):
        return sview(i * 2048, (i + 1) * 2048)

    def vn_t(i):
        a, k = sview(4096 + i * 1024, 4096 + (i + 1) * 1024)
        return a.bitcast(BF16), k

    def sq_v():
        a, k = sview(2048, 4096)
        return a.bitcast(BF16).rearrange("p (k n) -> p k n", k=8), k

    def rstd_v(i):
        return sview(4096 + i * 512, 4096 + (i + 1) * 512)

    def relu_v(i):
        return sview(i * 512, (i + 1) * 512)

    def e32_v():
        return sview(0, 1024)

    def e_v(i):
        a, k = sview(1024 + i * 512, 1024 + (i + 1) * 512)
        return a.bitcast(BF16), k

    def rden_v(i):
        return sview(2048 + i * 256, 2048 + (i + 1) * 256)

    def dma(eng, key, out, in_, reads=(), writes=()):
        P.add(eng, lambda e, out=out, in_=in_: e.dma_start(out=out, in_=in_),
              reads=reads, writes=writes, dma=key)

    CONST_SEM = "cst"
    dma("sp", CONST_SEM, IDENT[:, :], ident_d[:, :], writes=[("c", "ident")])
    dma("sp", CONST_SEM + "1", UTRI[:, :], utri_d[:, :], writes=[("c", "utri")])
    dma("sp", CONST_SEM + "2", GST[:, :], gstack_d[:, :], writes=[("c", "gst")])
    dma("sp", CONST_SEM + "3", EXPB[:, :, :].rearrange("p a b -> p (a b)"), biasg_d[:, :],
        writes=[("c", "expb")])
    am_ap, am_k = sview(0, 256)
    dma("sp", CONST_SEM + "4", am_ap, amask_d[:, :], writes=am_k)
    dma("sp", CONST_SEM + "5", ES[:, :, :].rearrange("p a b -> p (a b)"),
        b_sinks_d[:, :].rearrange("l h -> (l h)").partition_broadcast(128), writes=[("c", "es")])
    bsrc = a_b_sp_d[:, :, :].rearrange("l g i -> (l g i)")
    BT_PENDING = bsrc

    P.add("pool", lambda e: e.memset(ONES[:, :], 1.0), writes=[("c", "ones")])
    P.add("pool", lambda e: e.memset(ONESK[:, :], 1.0 / 1024.0), writes=[("c", "onesk")])
    P.add("pool", lambda e: e.memset(E2[:, :], 0.0), writes=[("c", "e2")])
    P.add("pool", lambda e: e.memset(E2[0:2, :], 1.0), writes=[("c", "e2")])
    P.add("pool", lambda e: e.memset(RB[:, :, :, :], 0.0), writes=[("c", "rb")])
    P.add("pool", lambda e: e.memset(KTP[:, :, :], 0.0), writes=[("KTp",)])
    P.add("pool", lambda e: e.memset(VP[:, :], 0.0), writes=[("Vp",)])
    P.add("pool", lambda e: e.memset(VC[:, :, :], 0.0), writes=[("V", 0), ("V", 1)])
    P.add("pool", lambda e: e.memset(HONES[:, :], 0.0), writes=[("c", "hones")])
    P.add("pool", lambda e: e.memset(HONES[:, 64:128], 1.0), writes=[("c", "hones")])

    P.add("act", lambda e: e.activation(out=ES[:, :, :], in_=ES[:, :, :], func=AF.Exp),
          reads=[("c", "es")], writes=[("c", "es")])
    ES5 = ES[:, :, :].rearrange("p l (g h r) -> p l g h r", g=4, h=2)
    for par in range(2):
        P.add("dve", lambda e, par=par: e.tensor_copy(out=ESC[par * 64:(par + 1) * 64, :, :, :],
                                                      in_=ES5[par * 64:(par + 1) * 64, :, :, :, par]),
              reads=[("c", "es")], writes=[("c", "esc")])
    P.add("act", lambda e: e.activation(out=EXPB[:, :, :], in_=EXPB[:, :, :], func=AF.Exp),
          reads=[("c", "expb")], writes=[("c", "expb")])
    for g16 in range(8):
        def fn(e, g16=g16):
            m = am_ap.rearrange("p (c e) -> p c e", c=2)
            ins = None
            for hh in range(2):
                v = EXPB[:, g16, :].rearrange("p (c d e) -> p c d e", c=2, d=2)[:, :, hh, :]
                ins = e.tensor_tensor(out=v, in0=v, in1=m, op=ALU.mult)
            return ins
        P.add("pool", fn, reads=[("c", "expb")] + am_k, writes=[("c", "expb")])

    pst, psk = bank()
    P.add("pe", lambda e, pst=pst: e.transpose(out=pst[:, 0:80], in_=GST[:, :], identity=IDENT[0:80, 0:80]),
          reads=[("c", "gst"), ("c", "ident")], writes=[psk])
    P.add("dve", lambda e, pst=pst: e.tensor_copy(out=GT[:, :], in_=pst[:, 0:80]),
          reads=[psk], writes=[("c", "gt")])
    G_MIX, G_FFN, G_KV, G_FIN = 0, 32, 64, 72

    RBflat = RB[:, :, :, :].rearrange("p a b c -> p (a b c)")
    _bt, btk = sview(2048, 6144)
    BTMP = _bt[0:1, :].rearrange("p (a b) -> p a b", a=2)
    _lb, lbk = sview(1024, 2048)
    LOB = _lb.bitcast(BF16)
    dma("sp", CONST_SEM + "6", BTMP[0:1, 0, :], BT_PENDING.partition_broadcast(1), writes=btk)
    P.add("dve", lambda e: e.tensor_copy(out=RBflat[0:1, :], in_=BTMP[0:1, 0, :]),
          reads=btk + [("c", "rb")], writes=[("c", "rb")])
    P.add("dve", lambda e: e.tensor_copy(out=BTMP[0:1, 1, :], in_=RBflat[0:1, :]),
          reads=[("c", "rb")], writes=btk)
    P.add("dve", lambda e: e.tensor_tensor(out=LOB[0:1, :], in0=BTMP[0:1, 0, :], in1=BTMP[0:1, 1, :],
                                           op=ALU.subtract),
          reads=btk, writes=lbk)
    dma("sp", CONST_SEM + "7", RBflat[1:2, :], LOB[0:1, :], reads=lbk, writes=[("c", "rb")])

    for l in range(2):
        if l not in layers:
            continue
        for g in range(8):
            xi, xk = xin(g % 2)
            dma("sp", "xin%d" % (g % 2), xi[:, 0:128], a_w_sp_d[l, g, :, :], writes=xk)
            pst, psk = bank()
            P.add("pe", lambda e, pst=pst, xi=xi: e.transpose(out=pst[:, 0:128], in_=xi[:, 0:128],
                                                             identity=IDENT[:, :]),
                  reads=xk + [("c", "ident")], writes=[psk])
            P.add("dve", lambda e, pst=pst, l=l, g=g: e.tensor_tensor(
                out=WST[:, l, g, :], in0=pst[:, 0:128], in1=UTRI[:, :], op=ALU.mult),
                reads=[psk, ("c", "utri")], writes=[("c", "wst")])

    pieces = []

    def kview(k):
        return lambda sl: RING[:, sl, :].rearrange("p (k n) -> p k n", k=k)

    for T in range(n_tiles):
        for l in layers:
            if l < 2:
                w = a_w_in_d[l].rearrange("(k p) n -> p k n", p=128)
                for ub in range(4):
                    pieces.append((("u", T, l, ub), [(kview(8), w[:, :, ub * 512:(ub + 1) * 512])]))
                for vb in range(4):
                    pieces.append((("v", T, l, vb), [(kview(8), w[:, :, 2048 + vb * 512:2048 + (vb + 1) * 512])]))
                wo = a_w_out_d[l].rearrange("(k p) n -> p k n", p=128)
                for ob in range(4):
                    pieces.append((("ao", T, l, ob), [(kview(16), wo[:, :, ob * 256:(ob + 1) * 256])]))
            else:
                j = l - 2
                if l == 2:
                    wk = w_k_d.rearrange("(k p) (kv d) -> p k kv d", p=128, d=64)
                    wv = w_v_d.rearrange("(k p) (kv d) -> p k kv d", p=128, d=64)
                    lst = []
                    for dup in range(2):
                        for kv in range(4):
                            lst.append((lambda sl, dup=dup, kv=kv: RING[:, sl, :].rearrange(
                                "p (k kv u d) -> p k kv u d", k=8, kv=4, u=2)[:, :, kv, dup, :],
                                wk[:, :, kv, :]))
                    pieces.append((("wk", T, l, 0), lst))
                    pieces.append((("wv", T, l, 0), [
                        (lambda sl: RING[:, sl, :].rearrange("p (k n) -> p k n", k=8)[:, :, 0:256],
                         w_v_d.rearrange("(k p) n -> p k n", p=128))]))
                wq = b_w_q_d[j].rearrange("(k p) n -> p k n", p=128)
                for qb in range(2):
                    pieces.append((("q", T, l, qb), [(kview(8), wq[:, :, qb * 512:(qb + 1) * 512])]))
                wo = b_w_o_d[j].rearrange("(k p) n -> p k n", p=128)
                for ob in range(2):
                    pieces.append((("bo", T, l, ob), [(kview(8), wo[:, :, ob * 512:(ob + 1) * 512])]))
            w1 = ffn_w1_d[l].rearrange("(k p) n -> p k n", p=128)
            for hf in range(2):
                for fb in range(4):
                    c0 = (hf * 4 + fb) * 512
                    pieces.append((("w1", T, l, hf * 4 + fb), [(kview(8), w1[:, :, c0:c0 + 512])]))
                w2 = ffn_w2_d[l][hf * 2048:(hf + 1) * 2048, :].rearrange("(k p) n -> p k n", p=128)
                for ob in range(4):
                    pieces.append((("w2", T, l, hf * 4 + ob), [(kview(16), w2[:, :, ob * 256:(ob + 1) * 256])]))

    st_w = {"next_load": 0, "next_use": 0}

    def acquire(n, tags):
        lo = st_w["next_use"]
        for i in range(n):
            assert pieces[lo + i][0][0] == tags[i][0] and pieces[lo + i][0][2:] == tags[i][1:], \
                (pieces[lo + i][0], tags[i])
        st_w["next_use"] = lo + n
        while st_w["next_load"] < min(lo + NSLOT, len(pieces)):
            pi = st_w["next_load"]
            sl = pi % NSLOT
            for (vf, src) in pieces[pi][1]:
                dma("pool", "slot%d" % sl, vf(sl), src, writes=[("slot", sl)])
            st_w["next_load"] = pi + 1
        return [((lo + i) % NSLOT) for i in range(n)]

    def mm_group(out_ap, pairs, reads, pkey):
        def fn(pe):
            n = len(pairs)
            ins = None
            for i, (l_, r_) in enumerate(pairs):
                ins = pe.matmul(out_ap, l_, r_, start=(i == 0), stop=(i == n - 1))
            return ins
        P.add("pe", fn, reads=reads, writes=[pkey])
        P.tick()

    def sub(s):
        return slice(s * 512, (s + 1) * 512)

    norm_state = {"done": False}

    def norm_A(s):
        sqv, sqk = sq_v()
        P.add("act", lambda e, s=s: e.activation(out=sqv, in_=XT[:, :, sub(s)], func=AF.Square),
              reads=[("X", c, s) for c in range(8)], writes=sqk)

    def norm_B(s, gcols, outs, exact=False):
        sqv, sqk = sq_v()
        pst, psk = bank()
        P.add("pe", lambda pe, pst=pst: [pe.matmul(pst[:, :], ONESK[:, :], sqv[:, k, :], start=(k == 0), stop=(k == 7))
                                         for k in range(8)][-1],
              reads=sqk + [("c", "onesk")], writes=[psk])
        rs, rk = rstd_v(s % 2)
        if exact or not OPT_LNEXP:
            P.add("act", lambda e, pst=pst, rs=rs: e.activation(
                out=rs, in_=pst[:, :], func=AF.Sqrt, bias=EPS, scale=1.0), reads=[psk], writes=rk)
            P.add("dve", lambda e, rs=rs: e.reciprocal(out=rs, in_=rs), reads=rk, writes=rk)
        else:
            P.add("act", lambda e, pst=pst, rs=rs: e.activation(
                out=rs, in_=pst[:, :], func=AF.Ln, bias=EPS, scale=1.0), reads=[psk], writes=rk)
            P.add("act", lambda e, rs=rs: e.activation(out=rs, in_=rs, func=AF.Exp, scale=-0.5),
                  reads=rk, writes=rk)
        cnt = 0
        for gi, gc in enumerate(gcols):
            tens, coff, kn = outs[gi]
            for k in range(8):
                if exact or k < 5 or not OPT_HNSPLIT:
                    P.add("dve", lambda e, k=k, s=s, rs=rs, gc=gc, tens=tens, coff=coff:
                          e.scalar_tensor_tensor(out=tens[:, coff + k, sub(s)], in0=XT[:, k, sub(s)],
                                                 scalar=GT[:, gc + k:gc + k + 1], in1=rs,
                                                 op0=ALU.mult, op1=ALU.mult),
                          reads=[("X", k, s), ("c", "gt")] + rk, writes=[(kn, coff + k, s)])
                else:
                    tm, tmk = sview(5120 + (cnt % 2) * 512, 5120 + (cnt % 2 + 1) * 512)
                    cnt += 1
                    P.add("act", lambda e, k=k, s=s, gc=gc, tm=tm: e.activation(
                        out=tm, in_=XT[:, k, sub(s)], func=AF.Identity, scale=GT[:, gc + k:gc + k + 1]),
                        reads=[("X", k, s), ("c", "gt")], writes=tmk)
                    P.add("pool", lambda e, k=k, s=s, rs=rs, tm=tm, tens=tens, coff=coff: e.tensor_tensor(
                        out=tens[:, coff + k, sub(s)], in0=tm, in1=rs, op=ALU.mult),
                        reads=tmk + rk, writes=[(kn, coff + k, s)])

    def rms_norm(gcols, outs):
        if norm_state["done"]:
            norm_state["done"] = False
            return
        for s in range(NSUB):
            norm_A(s)
            norm_B(s, gcols, outs, exact=(outs[0][0] is XT))

    def norm_spec_mix(l):
        if l == 2:
            return [G_KV, G_MIX + l * 8], [(BIG, 8, "big"), (HN, 0, "hn")]
        return [G_MIX + l * 8], [(HN, 0, "hn")]

    def norm_spec_ffn(l):
        return [G_FFN + l * 8], [(HN, 0, "hn")]

    def proj_fm(tagname, T, l, nblk, blk0, in_tens, in_off, in_key, nk, evac):
        ncol = 4096 // nk // 128

        def groups(sl, b, s):
            wv_ = RING[:, sl, :].rearrange("p (k n) -> p k n", k=nk)
            for c4 in range(ncol):
                pst, psk = bank()
                mm_group(pst[:, :],
                         [(wv_[:, k, c4 * 128:(c4 + 1) * 128], in_tens[:, in_off + k, sub(s)])
                          for k in range(nk)],
                         reads=[("slot", sl)] + [(in_key, in_off + k, s) for k in range(nk)], pkey=psk)
                evac(pst, psk, (blk0 + b) * ncol + c4, s)

        sls_ = acquire(2, [(tagname, l, blk0), (tagname, l, blk0 + 1)])
        for s in range(NSUB):
            for bi in range(2):
                groups(sls_[bi], bi, s)
        for b in range(2, nblk):
            (sl,) = acquire(1, [(tagname, l, blk0 + b)])
            for s in range(NSUB):
                groups(sl, b, s)

    def proj_fm_souter(tagname, T, l, nblk, blk0, in_tens, in_off, in_key, nk, evac, next_norm=None,
                       pre_hook=None):
        sls_ = acquire(nblk, [(tagname, l, blk0 + b) for b in range(nblk)])
        ncol = 4096 // nk // 128
        ngrp = [0]
        for s in range(NSUB):
            for b in range(nblk):
                wv_ = RING[:, sls_[b], :].rearrange("p (k n) -> p k n", k=nk)
                for c4 in range(ncol):
                    pst, psk = bank()
                    mm_group(pst[:, :],
                             [(wv_[:, k, c4 * 128:(c4 + 1) * 128], in_tens[:, in_off + k, sub(s)])
                              for k in range(nk)],
                             reads=[("slot", sls_[b])] + [(in_key, in_off + k, s) for k in range(nk)], pkey=psk)
                    evac(pst, psk, (blk0 + b) * ncol + c4, s)
                    if next_norm is not None:
                        cq = (blk0 + b) * ncol + c4 - (blk0 * ncol)
                        sqv_, _ = sq_v()
                        _, sqkc = sview(2048 + cq * 256, 2048 + (cq + 1) * 256)
                        P.add("act", lambda e, cq=cq, s=s, sqv_=sqv_: e.activation(
                            out=sqv_[:, cq, :], in_=XT[:, cq, sub(s)], func=AF.Square),
                            reads=[("X", cq, s)], writes=sqkc)
                    ngrp[0] += 1
                    if pre_hook is not None and ngrp[0] == pre_hook[0]:
                        pre_hook[1]()
            if next_norm is not None:
                gcols, outs = next_norm
                P.defer(1 if s == 0 else 2,
                        lambda s=s, gcols=gcols, outs=outs: norm_B(s, gcols, outs, exact=(outs[0][0] is XT)))
        if next_norm is not None:
            norm_state["done"] = True

    def evac_resid(pst, psk, c, s):
        P.add("dve", lambda e: e.tensor_tensor(out=XT[:, c, sub(s)], in0=XT[:, c, sub(s)],
                                               in1=pst[:, :], op=ALU.add),
              reads=[psk, ("X", c, s)], writes=[("X", c, s)])

    def ffn(T, l, nxt=None):
        rms_norm([G_FFN + l * 8], [(HN, 0, "hn")])
        for hf in range(2):
            def evac_relu2(pst, psk, c, s, hf=hf):
                cc = c - hf * 16
                rv, rk = relu_v((cc + s) % 2)
                P.add("act", lambda e: e.activation(out=rv, in_=pst[:, :], func=AF.Relu),
                      reads=[psk], writes=rk)
                P.add("pool", lambda e: e.tensor_tensor(out=BIG[:, cc, sub(s)], in0=rv, in1=rv, op=ALU.mult),
                      reads=rk, writes=[("big", cc, s)])
            proj_fm("w1", T, l, 4, hf * 4, HN, 0, "hn", 8, evac_relu2)

            def evac_r2(pst, psk, c, s, hf=hf):
                evac_resid(pst, psk, c - hf * 8, s)
            if hf == 1:
                proj_fm_souter("w2", T, l, 4, hf * 4, BIG, 0, "big", 16, evac_r2, next_norm=nxt)
            else:
                proj_fm("w2", T, l, 4, hf * 4, BIG, 0, "big", 16, evac_r2)

    def mixer_a(T, l):
        rms_norm([G_MIX + l * 8], [(HN, 0, "hn")])
        dma("sp", "lng", LNG, a_ln_g_d[l:l + 1, :].partition_broadcast(128)[:, 0],
            writes=[("KT", 0), ("KT", 1)])

        def evac_gelu(pst, psk, c, s):
            P.add("act", lambda e: e.activation(out=BIG[:, c, sub(s)], in_=pst[:, :], func=AF.Gelu),
                  reads=[psk], writes=[("big", c, s)])
        proj_fm("u", T, l, 4, 0, HN, 0, "hn", 8, evac_gelu)

        sls = acquire(4, [("v", l, vb) for vb in range(4)])

        def v_stage(t):
            s = t // 4
            tsl = slice(t * 128, (t + 1) * 128)
            vt, vk = v_t(t % 2)
            vn, vnk = vn_t(t % 2)
            skey = ("stat", t % 2)
            mvk = ("mv", t % 2)
            for vb in range(4):
                wv_ = RING[:, sls[vb], :].rearrange("p (k n) -> p k n", k=8)
                pst, psk = bank()
                mm_group(pst[:, :], [(HN[:, k, tsl], wv_[:, k, :]) for k in range(8)],
                         reads=[("slot", sls[vb])] + [("hn", k, s) for k in range(8)], pkey=psk)
                P.add("act", lambda e, pst=pst, vt=vt, vb=vb: e.activation(
                    out=vt[:, vb * 512:(vb + 1) * 512], in_=pst[:, :], func=AF.Gelu),
                    reads=[psk], writes=vk[vb * 2:(vb + 1) * 2])
                P.add("dve", lambda e, vt=vt, vb=vb, t=t: e.bn_stats(
                    out=STAT[:, t % 2, vb, :], in_=vt[:, vb * 512:(vb + 1) * 512]),
                    reads=vk[vb * 2:(vb + 1) * 2], writes=[skey])
            P.add("dve", lambda e, t=t: e.bn_aggr(out=MV[:, t % 2, 0:2],
                                                  in_=STAT[:, t % 2, :, :].rearrange("p a b -> p (a b)")),
                  reads=[skey], writes=[mvk])
            P.add("act", lambda e, t=t: e.activation(out=MV[:, t % 2, 1:2], in_=MV[:, t % 2, 1:2],
                                                     func=AF.Sqrt, bias=EPS, scale=1.0),
                  reads=[mvk], writes=[mvk])
            P.add("dve", lambda e, t=t: e.reciprocal(out=MV[:, t % 2, 1:2], in_=MV[:, t % 2, 1:2]),
                  reads=[mvk], writes=[mvk])
            P.add("dve", lambda e, t=t: e.scalar_tensor_tensor(
                out=MV[:, t % 2, 2:3], in0=MV[:, t % 2, 0:1], scalar=-1.0, in1=MV[:, t % 2, 1:2],
                op0=ALU.mult, op1=ALU.mult), reads=[mvk], writes=[mvk])
            for half, eng in ((0, "pool"), (1, "dve" if OPT_LNHALF else "pool")):
                hs = slice(half * 1024, (half + 1) * 1024)
                vkh = vk[half * 4:(half + 1) * 4]
                vnkh = vnk[half * 2:(half + 1) * 2]
                P.add("act", lambda e, t=t, vt=vt, hs=hs: e.activation(
                    out=vt[:, hs], in_=vt[:, hs], func=AF.Identity, bias=MV[:, t % 2, 2:3], scale=MV[:, t % 2, 1:2]),
                    reads=[mvk] + vkh, writes=vkh)
                P.add(eng, lambda e, vt=vt, vn=vn, hs=hs: e.tensor_tensor(out=vn[:, hs], in0=vt[:, hs], in1=LNG[:, hs],
                                                                         op=ALU.mult),
                      reads=vkh + [("KT", 0), ("KT", 1)], writes=vnkh)

        def sp_stage(t):
            s = t // 4
            tsl = slice(t * 128, (t + 1) * 128)
            vn, vnk = vn_t(t % 2)
            for c0 in range(0, 16, 4):
                pst, psk = bank()

                def fn(pe, pst=pst, c0=c0, vn=vn):
                    ins = None
                    for ci in range(4):
                        c = c0 + ci
                        g = c // 2
                        o_ = pst[:, ci * 128:(ci + 1) * 128]
                        pe.matmul(o_, vn[:, c * 128:(c + 1) * 128], WST[:, l, g, :], start=True, stop=False)
                        ins = pe.matmul(o_, E2[:, :], RB[:, l, g, :], start=False, stop=True)
                    return ins
                P.add("pe", fn, reads=vnk + [("c", "wst"), ("c", "rb"), ("c", "e2")], writes=[psk])
                P.add("dve", lambda e, pst=pst, c0=c0, tsl=tsl: e.tensor_tensor(
                    out=BIG[:, c0:c0 + 4, tsl], in0=pst[:, :].rearrange("p (a b) -> p a b", a=4),
                    in1=BIG[:, c0:c0 + 4, tsl], op=ALU.mult),
                    reads=[psk] + [("big", c0 + ci, s) for ci in range(4)],
                    writes=[("big", c0 + ci, s) for ci in range(4)])

        v_stage(0)
        for t in range(NCH - 1):
            v_stage(t + 1)
            sp_stage(t)
        proj_fm_souter("ao", T, l, 4, 0, BIG, 0, "big", 16, evac_resid, next_norm=norm_spec_ffn(l),
                       pre_hook=(4, lambda: sp_stage(NCH - 1)))

    def mixer_b(T, l):
        j = l - 2
        if l == 2:
            rms_norm(*norm_spec_mix(2))
            (slk,) = acquire(1, [("wk", l, 0)])
            wkv = RING[:, slk, :].rearrange("p (k n) -> p k n", k=8)
            for s in range(NSUB):
                for kv in range(4):
                    pst, psk = bank()
                    mm_group(pst[:, :], [(wkv[:, k, kv * 128:(kv + 1) * 128], BIG[:, 8 + k, sub(s)])
                                         for k in range(8)],
                             reads=[("slot", slk)] + [("big", 8 + k, s) for k in range(8)], pkey=psk)
                    P.add("act", lambda e, pst=pst, kv=kv, s=s: e.activation(
                        out=KTC[:, kv, sub(s)], in_=pst[:, :], func=AF.Copy),
                        reads=[psk], writes=[("KT", s)])
            (slv,) = acquire(1, [("wv", l, 0)])
            wvv = RING[:, slv, :].rearrange("p (k n) -> p k n", k=8)
            for t in range(NCH):
                s = t // 4
                pst, psk = bank()
                mm_group(pst[:, 0:256], [(BIG[:, 8 + k, t * 128:(t + 1) * 128], wvv[:, k, 0:256]) for k in range(8)],
                         reads=[("slot", slv)] + [("big", 8 + k, s) for k in range(8)], pkey=psk)
                P.add("act", lambda e, pst=pst, t=t: e.activation(
                    out=VC[:, t, :].rearrange("p (g c) -> p g c", g=4)[:, :, 64:128],
                    in_=pst[:, 0:256].rearrange("p (g c) -> p g c", g=4), func=AF.Copy),
                    reads=[psk], writes=[("V", s)])
        else:
            rms_norm([G_MIX + l * 8], [(HN, 0, "hn")])

        def evac_q(pst, psk, c, s):
            P.add("act", lambda e: e.mul(out=BIG[:, c, sub(s)], in_=pst[:, :], mul=0.125),
                  reads=[psk], writes=[("big", c, s)])
        proj_fm("q", T, l, 2, 0, HN, 0, "hn", 8, evac_q)

        its = [(t, kvg) for t in range(NCH) for kvg in range(4)]

        def jcs_of(t):
            return [1] if (T * NCH + t) == 0 else [0, 1]

        def e32_rng(n, par, lo):
            base = (0, 3072, 4096)[n % 3] + par * 512
            return sview(base + lo, base + 512)

        def eb_rng(n, par, lo):
            base = (1024, 1536, 2560)[n % 3] + par * 256
            a, k = sview(base, base + 256)
            return a.bitcast(BF16), k

        def stage_qk(n):
            t, kvg = its[n]
            s = t // 4
            tsl = slice(t * 128, (t + 1) * 128)
            jcs = jcs_of(t)
            lo = 0 if 0 in jcs else 256
            for par in range(2):
                psl = slice(par * 64, (par + 1) * 64)
                pst, psk = bank()

                def fn(pe, pst=pst, psl=psl, kvg=kvg, t=t, tsl=tsl, jcs=jcs):
                    ins = None
                    for jc in jcs:
                        if jc == 1:
                            kt = KTC[psl, kvg, tsl]
                        elif t == 0:
                            kt = KTP[psl, kvg, :]
                        else:
                            kt = KTC[psl, kvg, (t - 1) * 128:t * 128]
                        o_ = pst[:, jc * 256:(jc + 1) * 256].rearrange("p (d e) -> p d e", d=2)
                        ins = pe.matmul(o_, kt, BIG[psl, 2 * kvg:2 * kvg + 2, tsl], start=True, stop=True)
                    return ins
                rds = [("big", 2 * kvg, s), ("big", 2 * kvg + 1, s), ("KT", s)]
                if 0 in jcs:
                    rds.append(("KTp",) if t == 0 else ("KT", (t - 1) // 4))
                P.add("pe", fn, reads=rds, writes=[psk])
                ea, ek = e32_rng(n, par, lo)
                P.add("act", lambda e, pst=pst, lo=lo, ea=ea: e.activation(
                    out=ea, in_=pst[:, lo:512], func=AF.Exp), reads=[psk], writes=ek)

        def stage_mul(n):
            t, kvg = its[n]
            lo = 0 if 0 in jcs_of(t) else 256
            for par, eng in ((0, "pool"), (1, "pool")):
                ea, ek = e32_rng(n, par, lo)
                ba, bk = eb_rng(n, par, 0)
                P.add(eng, lambda e, ea=ea, ba=ba, kvg=kvg, par=par, lo=lo: e.tensor_tensor(
                    out=ba[:, lo:512], in0=ea, in1=EXPB[:, kvg * 2 + par, lo:512], op=ALU.mult),
                    reads=ek + [("c", "expb")], writes=bk)

        def stage_pv(n):
            t, kvg = its[n]
            s = t // 4
            tsl = slice(t * 128, (t + 1) * 128)
            jcs = jcs_of(t)
            ebs = [eb_rng(n, par, 0) for par in range(2)]
            NDb, NDk = bank()

            def fnpv(pe, kvg=kvg, t=t, ebs=ebs, NDb=NDb, jcs=jcs):
                ins = None
                for use_v in (True, False):
                    o_ = NDb[:, (0 if use_v else 256):(256 if use_v else 512)].rearrange("p (d e) -> p d e", d=2)
                    mms = [(par, jc) for par in range(2) for jc in jcs]
                    for mi, (par, jc) in enumerate(mms):
                        ev = ebs[par][0].rearrange("p (c d e) -> p c d e", c=2, d=2)
                        c0 = 64 if par == 0 else 0
                        if not use_v:
                            lhs = HONES[:, c0:c0 + 128]
                        elif jc == 1:
                            lhs = VC[:, t, kvg * 192 + c0:kvg * 192 + c0 + 128]
                        elif t == 0:
                            lhs = VP[:, kvg * 192 + c0:kvg * 192 + c0 + 128]
                        else:
                            lhs = VC[:, t - 1, kvg * 192 + c0:kvg * 192 + c0 + 128]
                        ins = pe.matmul(o_, lhs, ev[:, jc, :, :], start=(mi == 0), stop=(mi == len(mms) - 1))
                return ins
            rds = ebs[0][1] + ebs[1][1] + [("V", s), ("c", "hones")]
            if 0 in jcs:
                rds.append(("Vp",) if t == 0 else ("V", (t - 1) // 4))
            P.add("pe", fnpv, reads=rds, writes=[NDk])
            rd, rdk = rden_v(n % 2)
            P.add("dve", lambda e, NDb=NDb, rd=rd, kvg=kvg: e.tensor_scalar(
                out=rd[:, 0:128], in0=NDb[:, 256:384], scalar1=ESC[:, j, kvg, 0:1], scalar2=None, op0=ALU.add),
                reads=[NDk, ("c", "esc")], writes=[rdk[0], ("rdh", n % 2, 0)])
            P.add("dve", lambda e, rd=rd: e.reciprocal(out=rd[:, 0:128], in_=rd[:, 0:128]),
                  reads=[("rdh", n % 2, 0)], writes=[("rdh", n % 2, 0)])
            P.add("act", lambda e, NDb=NDb, rd=rd, kvg=kvg: e.activation(
                out=rd[:, 128:256], in_=NDb[:, 384:512], func=AF.Ln, bias=ESC[:, j, kvg, 1:2], scale=1.0),
                reads=[NDk, ("c", "esc")], writes=[rdk[0], ("rdh", n % 2, 1)])
            P.add("act", lambda e, rd=rd: e.activation(out=rd[:, 128:256], in_=rd[:, 128:256], func=AF.Exp, scale=-1.0),
                  reads=[("rdh", n % 2, 1)], writes=[("rdh", n % 2, 1)])
            P.add("dve", lambda e, NDb=NDb, rd=rd, kvg=kvg, tsl=tsl: e.tensor_tensor(
                out=BIG[:, 8 + 2 * kvg:8 + 2 * kvg + 2, tsl],
                in0=NDb[:, 0:256].rearrange("p (d e) -> p d e", d=2),
                in1=rd.rearrange("p (d e) -> p d e", d=2), op=ALU.mult),
                reads=[NDk, ("rdh", n % 2, 0), ("rdh", n % 2, 1)],
                writes=[("big", 8 + 2 * kvg, s), ("big", 8 + 2 * kvg + 1, s)])

        for n0 in range(OPT_DEPTH):
            stage_qk(n0)
        for n in range(len(its)):
            if n + OPT_DEPTH < len(its):
                stage_qk(n + OPT_DEPTH)
            stage_mul(n)
            stage_pv(n)
        if debug == "attn":
            dbg_d = nc.dram_tensor("dbg", [128, 16 * TT], BF16, kind="ExternalOutput").ap()
            dma("sp", "dbg", dbg_d[:, :], BIG[:, :, :].rearrange("p a b -> p (a b)"),
                reads=[("big", c, s) for c in range(16) for s in range(NSUB)])
            raise StopIteration
        proj_fm_souter("bo", T, l, 2, 0, BIG, 8, "big", 8, evac_resid, next_norm=norm_spec_ffn(l))
        if l == 3 or (3 not in layers):
            P.add("pool", lambda e: e.tensor_copy(out=KTP[:, :, :], in_=KTC[:, :, TT - 128:TT]),
                  reads=[("KT", 1)], writes=[("KTp",)])
            P.add("pool", lambda e: e.tensor_copy(out=VP[:, :], in_=VC[:, NCH - 1, :]),
                  reads=[("V", 1)], writes=[("Vp",)])

    for T in range(n_tiles):
        tok0 = T * TT
        for t in range(NCH):
            xi, xk = xin(t % 4)
            dma("sp", "xin%d" % (t % 4), xi, x_d[tok0 + t * 128:tok0 + (t + 1) * 128, :], writes=xk)
            for cg in range(2):
                pst, psk = bank()

                def fn(pe, pst=pst, xi=xi, cg=cg):
                    ins = None
                    for ci in range(4):
                        c = cg * 4 + ci
                        ins = pe.transpose(out=pst[:, ci * 128:(ci + 1) * 128], in_=xi[:, c * 128:(c + 1) * 128],
                                           identity=IDENT[:, :])
                    return ins
                P.add("pe", fn, reads=xk + [("c", "ident")], writes=[psk])
                P.add("act", lambda e, pst=pst, cg=cg, t=t: e.activation(
                    out=XT[:, cg * 4:(cg + 1) * 4, t * 128:(t + 1) * 128],
                    in_=pst[:, :].rearrange("p (a b) -> p a b", a=4), func=AF.Copy),
                    reads=[psk], writes=[("X", cg * 4 + ci, t // 4) for ci in range(4)])
        try:
            for l in layers:
                if l < 2:
                    mixer_a(T, l)
                else:
                    mixer_b(T, l)
                if debug == "noffn":
                    for _ in range(16):
                        acquire(1, [pieces[st_w["next_use"]][0][0:1] + pieces[st_w["next_use"]][0][2:]])
                    continue
                li = layers.index(l)
                if li + 1 < len(layers):
                    nxt = norm_spec_mix(layers[li + 1])
                elif final_norm:
                    nxt = ([G_FIN], [(XT, 0, "X")])
                else:
                    nxt = None
                ffn(T, l, nxt)
        except StopIteration:
            st_w["next_use"] = len(pieces)
            break
        if final_norm:
            rms_norm([G_FIN], [(XT, 0, "X")])
        P.flush()
        for t in range(NCH):
            yo, yk = xin(t % 4)
            for cg in range(2):
                pst, psk = bank()

                def fn(pe, pst=pst, cg=cg, t=t):
                    ins = None
                    for ci in range(4):
                        c = cg * 4 + ci
                        ins = pe.transpose(out=pst[:, ci * 128:(ci + 1) * 128],
                                           in_=XT[:, c, t * 128:(t + 1) * 128], identity=IDENT[:, :])
                    return ins
                P.add("pe", fn, reads=[("X", cg * 4 + ci, t // 4) for ci in range(4)] + [("c", "ident")],
                      writes=[psk])
                P.add("act", lambda e, pst=pst, cg=cg, yo=yo: e.activation(
                    out=yo[:, cg * 512:(cg + 1) * 512], in_=pst[:, :], func=AF.Copy),
                    reads=[psk], writes=yk)
            dma("sp", "xin%d" % (t % 4), y_d[tok0 + t * 128:tok0 + (t + 1) * 128, :], yo, reads=yk)

    assert st_w["next_use"] == len(pieces), (st_w, len(pieces))

    P.finalize()
    keys = list(P.ENG) + sorted(P.dma_cnt.keys())
    with ExitStack() as es:
        sems = {k: es.enter_context(nc.semaphore("s_" + k)) for k in keys}
        block = es.enter_context(nc.Block())

        @block.tensor
        def _(e):
            P.emit("pe", e, sems)

        @block.scalar
        def _(e):
            P.emit("act", e, sems)

        @block.vector
        def _(e):
            P.emit("dve", e, sems)

        @block.gpsimd
        def _(e):
            P.emit("pool", e, sems)

        @block.sync
        def _(e):
            P.emit("sp", e, sems, final_waits=("xin0", "xin1", "xin2", "xin3", "dbg"))
    return nc


def host_consts(mix_norm_g, ffn_norm_g, kv_norm_g, final_norm_g, rel_bias):
    f = np.float32
    gstack = np.concatenate([np.asarray(mix_norm_g, f).reshape(32, 128),
                             np.asarray(ffn_norm_g, f).reshape(32, 128),
                             np.asarray(kv_norm_g, f).reshape(8, 128),
                             np.asarray(final_norm_g, f).reshape(8, 128)], axis=0)
    bucket, inwin = _bucket_table()
    rb = np.asarray(rel_bias, f)
    bias_full = np.transpose(rb[bucket], (2, 0, 1))
    bf = bias_full.reshape(4, 2, 2, 128, 2, 128)
    biasg = np.ascontiguousarray(np.transpose(bf, (5, 0, 2, 4, 1, 3))).reshape(128, 16 * 256)
    amask = np.ascontiguousarray(
        np.transpose(inwin.reshape(128, 2, 128), (2, 1, 0))).astype(f).reshape(128, 256)
    ident = np.eye(128, dtype=f)
    utri = np.triu(np.ones((128, 128), f))
    return dict(gstack=np.ascontiguousarray(gstack), biasg=biasg, amask=amask, ident=ident, utri=utri)


_NC_CACHE = {}


def kernel(x, mix_norm_g, ffn_norm_g, a_w_in, a_ln_g, a_w_spatial, a_b_spatial, a_w_out,
           kv_norm_g, w_k, w_v, b_w_q, b_sinks, b_w_o, rel_bias, ffn_w1, ffn_w2, final_norm_g):
    f = np.float32
    x = np.asarray(x, f)
    B = x.shape[0]
    if "nc" not in _NC_CACHE:
        _NC_CACHE["nc"] = build_nc()
    nc = _NC_CACHE["nc"]
    common = host_consts(mix_norm_g, ffn_norm_g, kv_norm_g, final_norm_g, rel_bias)
    for k, v in dict(a_w_in=a_w_in, a_ln_g=a_ln_g, a_w_spatial=a_w_spatial, a_b_spatial=a_b_spatial,
                     a_w_out=a_w_out, w_k=w_k, w_v=w_v, b_w_q=b_w_q, b_sinks=b_sinks, b_w_o=b_w_o,
                     ffn_w1=ffn_w1, ffn_w2=ffn_w2).items():
        common[k] = np.ascontiguousarray(np.asarray(v, f))
    in_maps = []
    for b in range(B):
        m = dict(common)
        m["x"] = np.ascontiguousarray(x[b])
        in_maps.append(m)
    res = run_bass_kernel_spmd(nc, in_maps, core_ids=list(range(B)))
    return np.stack([np.asarray(r["y"], f) for r in res.results], axis=0)
```

```python
import math
import os
from contextlib import ExitStack

import numpy as np
import concourse.bass as bass
import concourse.mybir as mybir
from concourse.bass_utils import run_bass_kernel_spmd

F32 = mybir.dt.float32
BF16 = mybir.dt.bfloat16
AF = mybir.ActivationFunctionType
ALU = mybir.AluOpType

SEQ = 4096
D = 1024
TT = 1024
NSUB = TT // 512
NCH = TT // 128
NSLOT = 6
EPS = 1e-6
OPT_HNSPLIT = int(os.environ.get("K_HNSPLIT", "0"))
OPT_LNEXP = int(os.environ.get("K_LNEXP", "1"))
OPT_DEPTH = int(os.environ.get("K_DEPTH", "2"))
OPT_LNHALF = int(os.environ.get("K_LNHALF", "0"))


class Op:
    __slots__ = ("eng", "fn", "dma", "deps", "signal", "done_key", "done_val")


class Prog:
    ENG = ("pe", "act", "dve", "pool", "sp")

    def __init__(self):
        self.eng_ops = {e: [] for e in self.ENG}
        self.res = {}
        self.dma_cnt = {}
        self.deferred = []

    def add(self, eng, fn, reads=(), writes=(), dma=None):
        o = Op()
        o.eng, o.fn, o.dma, o.signal = eng, fn, dma, False
        if dma is not None:
            c = self.dma_cnt.get(dma, 0) + 1
            self.dma_cnt[dma] = c
            o.done_key, o.done_val = dma, 16 * c
        else:
            o.done_key, o.done_val = eng, None
        deps = {}

        def need(d, raw, waw=False, strict=False):
            if d is o:
                return
            if waw and dma is not None and d.dma == dma:
                return
            if d.dma is None and d.eng == eng and dma is None:
                if eng == "pe" or not (raw or strict):
                    return
            deps[id(d)] = d

        res = self.res
        for r in reads:
            st = res.get(r)
            if st is None:
                st = res[r] = [None, {}]
            if st[0] is not None:
                need(st[0], True)
        for w in writes:
            st = res.get(w)
            if st is None:
                st = res[w] = [None, {}]
            strict = (w[0] == "c")
            if st[0] is not None:
                need(st[0], False, True, strict)
            for rd in st[1].values():
                need(rd, False, False, strict)
        for r in reads:
            res[r][1][o.done_key] = o
        for w in writes:
            st = res[w]
            st[0] = o
            st[1] = {}
        o.deps = list(deps.values())
        for d in o.deps:
            d.signal = True
        self.eng_ops[eng].append(o)
        return o

    def defer(self, n, fn):
        self.deferred.append([n, fn])

    def tick(self):
        due = [d for d in self.deferred if d[0] <= 1]
        self.deferred = [[d[0] - 1, d[1]] for d in self.deferred if d[0] > 1]
        for d in due:
            d[1]()

    def flush(self):
        due, self.deferred = self.deferred, []
        for d in due:
            d[1]()

    def finalize(self):
        for e in self.ENG:
            c = 0
            for o in self.eng_ops[e]:
                if o.dma is None and o.signal:
                    c += 1
                    o.done_val = c

    def emit(self, ename, eng, sems, final_waits=()):
        waited = {}
        for o in self.eng_ops[ename]:
            for d in o.deps:
                k, v = d.done_key, d.done_val
                if waited.get(k, 0) < v:
                    eng.wait_ge(sems[k], v)
                    waited[k] = v
            ins = o.fn(eng)
            if o.dma is not None:
                ins.then_inc(sems[o.dma], 16)
            elif o.signal:
                ins.then_inc(sems[ename], 1)
        for k in final_waits:
            v = 16 * self.dma_cnt.get(k, 0)
            if v:
                eng.wait_ge(sems[k], v)


def _bucket_table():
    qi = np.arange(128)[:, None]
    kj = np.arange(256)[None, :]
    dist = qi + 128 - kj
    d = np.maximum(dist, 0)
    lr = (np.log(np.maximum(d, 1).astype(np.float32) / np.float32(16)) /
          np.float32(math.log(128 / 16))).astype(np.float32)
    large = np.minimum(16 + (lr * np.float32(16)).astype(np.int32), 31)
    bucket = np.where(d < 16, d, large)
    inwin = (dist >= 0) & (dist < 128)
    return bucket, inwin


def build_nc(n_tiles=4, layers=(0, 1, 2, 3), final_norm=True, debug=None):
    nc = bass.Bass("TRN2", target_bir_lowering=False)
    P = Prog()

    def din(name, shape):
        return nc.dram_tensor(name, list(shape), F32, kind="ExternalInput").ap()

    x_d = din("x", (SEQ, D))
    gstack_d = din("gstack", (80, 128))
    a_w_in_d = din("a_w_in", (2, 1024, 4096))
    a_ln_g_d = din("a_ln_g", (2, 2048))
    a_w_sp_d = din("a_w_spatial", (2, 8, 128, 128))
    a_b_sp_d = din("a_b_spatial", (2, 8, 128))
    a_w_out_d = din("a_w_out", (2, 2048, 1024))
    w_k_d = din("w_k", (1024, 256))
    w_v_d = din("w_v", (1024, 256))
    b_w_q_d = din("b_w_q", (2, 1024, 1024))
    b_sinks_d = din("b_sinks", (2, 16))
    b_w_o_d = din("b_w_o", (2, 1024, 1024))
    biasg_d = din("biasg", (128, 16 * 256))
    ffn_w1_d = din("ffn_w1", (4, 1024, 4096))
    ffn_w2_d = din("ffn_w2", (4, 4096, 1024))
    ident_d = din("ident", (128, 128))
    utri_d = din("utri", (128, 128))
    amask_d = din("amask", (128, 256))
    y_d = nc.dram_tensor("y", [SEQ, D], F32, kind="ExternalOutput").ap()

    def sb(name, shape, dt):
        return nc.alloc_sbuf_tensor(name, list(shape), dt)

    XT = sb("XT", [128, 8, TT], F32)
    HN = sb("HN", [128, 8, TT], BF16)
    BIG = sb("BIG", [128, 16, TT], BF16)
    RING = sb("RING", [128, NSLOT, 4096], BF16)
    S = sb("S", [128, 6144], F32)
    KTC = sb("KTC", [128, 4, TT], BF16)
    VC = sb("VC", [128, 8, 768], BF16)
    KTP = sb("KTP", [128, 4, 128], BF16)
    VP = sb("VP", [128, 768], BF16)
    EXPB = sb("EXPB", [128, 8, 512], F32)
    RB = sb("RB", [128, 2, 8, 128], BF16)
    WST = sb("WST", [128, 2, 8, 128], BF16)
    IDENT = sb("IDENT", [128, 128], F32)
    UTRI = sb("UTRI", [128, 128], F32)
    ONES = sb("ONES", [128, 128], BF16)
    ONESK = sb("ONESK", [128, 128], BF16)
    E2 = sb("E2", [128, 128], BF16)
    GT = sb("GT", [128, 80], F32)
    ES = sb("ES", [128, 2, 16], F32)
    ESC = sb("ESC", [128, 2, 4, 2], F32)
    HONES = sb("HONES", [128, 192], BF16)
    STAT = sb("STAT", [128, 2, 4, 6], F32)
    MV = sb("MV", [128, 2, 4], F32)
    GST = sb("GST", [80, 128], F32)

    LNG = KTC[:, :, :].rearrange("p a b -> p (a b)").bitcast(F32)

    PS = [nc.alloc_psum_tensor("ps%d" % i, [128, 512], F32) for i in range(8)]
    ps_rr = [0]

    def bank():
        i = ps_rr[0]
        ps_rr[0] = (i + 1) % 8
        return PS[i], ("ps", i)

    def sview(lo, hi):
        return S[:, lo:hi], [("S", b) for b in range(lo // 256, (hi - 1) // 256 + 1)]

    def xin(i):
        return sview(i * 1024, (i + 1) * 1024)

    def v_t(i):
        return sview(i * 2048, (i + 1) * 2048)

    def vn_t(i):
        a, k = sview(4096 + i * 1024, 4096 + (i + 1) * 1024)
        return a.bitcast(BF16), k

    def sq_v():
        a, k = sview(2048, 4096)
        return a.bitcast(BF16).rearrange("p (k n) -> p k n", k=8), k

    def rstd_v(i):
        return sview(4096 + i * 512, 4096 + (i + 1) * 512)

    def relu_v(i):
        return sview(i * 512, (i + 1) * 512)

    def e32_v():
        return sview(0, 1024)

    def e_v(i):
        a, k = sview(1024 + i * 512, 1024 + (i + 1) * 512)
        return a.bitcast(BF16), k

    def rden_v(i):
        return sview(2048 + i * 256, 2048 + (i + 1) * 256)

    def dma(eng, key, out, in_, reads=(), writes=()):
        P.add(eng, lambda e, out=out, in_=in_: e.dma_start(out=out, in_=in_),
              reads=reads, writes=writes, dma=key)

    CONST_SEM = "cst"
    dma("sp", CONST_SEM, IDENT[:, :], ident_d[:, :], writes=[("c", "ident")])
    dma("sp", CONST_SEM + "1", UTRI[:, :], utri_d[:, :], writes=[("c", "utri")])
    dma("sp", CONST_SEM + "2", GST[:, :], gstack_d[:, :], writes=[("c", "gst")])
    dma("sp", CONST_SEM + "3", EXPB[:, :, :].rearrange("p a b -> p (a b)"), biasg_d[:, :],
        writes=[("c", "expb")])
    am_ap, am_k = sview(5120, 5376)
    dma("sp", CONST_SEM + "4", am_ap, amask_d[:, :], writes=am_k)
    dma("sp", CONST_SEM + "5", ES[:, :, :].rearrange("p a b -> p (a b)"),
        b_sinks_d[:, :].rearrange("l h -> (l h)").partition_broadcast(128), writes=[("c", "es")])
    bsrc = a_b_sp_d[:, :, :].rearrange("l g i -> (l g i)")
    BT_PENDING = bsrc

    P.add("pool", lambda e: e.memset(ONES[:, :], 1.0), writes=[("c", "ones")])
    P.add("pool", lambda e: e.memset(ONESK[:, :], 1.0 / 1024.0), writes=[("c", "onesk")])
    P.add("pool", lambda e: e.memset(E2[:, :], 0.0), writes=[("c", "e2")])
    P.add("pool", lambda e: e.memset(E2[0:2, :], 1.0), writes=[("c", "e2")])
    P.add("pool", lambda e: e.memset(RB[:, :, :, :], 0.0), writes=[("c", "rb")])
    P.add("pool", lambda e: e.memset(KTP[:, :, :], 0.0), writes=[("KTp",)])
    P.add("pool", lambda e: e.memset(VP[:, :], 0.0), writes=[("Vp",)])
    P.add("pool", lambda e: e.memset(VC[:, :, :], 0.0), writes=[("V", 0), ("V", 1)])
    P.add("pool", lambda e: e.memset(HONES[:, :], 0.0), writes=[("c", "hones")])
    P.add("pool", lambda e: e.memset(HONES[:, 64:128], 1.0), writes=[("c", "hones")])

    P.add("act", lambda e: e.activation(out=ES[:, :, :], in_=ES[:, :, :], func=AF.Exp),
          reads=[("c", "es")], writes=[("c", "es")])
    ES5 = ES[:, :, :].rearrange("p l (g h r) -> p l g h r", g=4, h=2)
    for par in range(2):
        P.add("dve", lambda e, par=par: e.tensor_copy(out=ESC[par * 64:(par + 1) * 64, :, :, :],
                                                      in_=ES5[par * 64:(par + 1) * 64, :, :, :, par]),
              reads=[("c", "es")], writes=[("c", "esc")])
    P.add("act", lambda e: e.activation(out=EXPB[:, :, :], in_=EXPB[:, :, :], func=AF.Exp),
          reads=[("c", "expb")], writes=[("c", "expb")])
    for g16 in range(8):
        def fn(e, g16=g16):
            m = am_ap.rearrange("p (c e) -> p c e", c=2)
            ins = None
            for hh in range(2):
                v = EXPB[:, g16, :].rearrange("p (c d e) -> p c d e", c=2, d=2)[:, :, hh, :]
                ins = e.tensor_tensor(out=v, in0=v, in1=m, op=ALU.mult)
            return ins
        P.add("pool", fn, reads=[("c", "expb")] + am_k, writes=[("c", "expb")])

    pst, psk = bank()
    P.add("pe", lambda e, pst=pst: e.transpose(out=pst[:, 0:80], in_=GST[:, :], identity=IDENT[0:80, 0:80]),
          reads=[("c", "gst"), ("c", "ident")], writes=[psk])
    P.add("dve", lambda e, pst=pst: e.tensor_copy(out=GT[:, :], in_=pst[:, 0:80]),
          reads=[psk], writes=[("c", "gt")])
    G_MIX, G_FFN, G_KV, G_FIN = 0, 32, 64, 72

    RBflat = RB[:, :, :, :].rearrange("p a b c -> p (a b c)")
    _bt, btk = sview(2048, 6144)
    BTMP = _bt[0:1, :].rearrange("p (a b) -> p a b", a=2)
    _lb, lbk = sview(1024, 2048)
    LOB = _lb.bitcast(BF16)
    dma("sp", CONST_SEM + "6", BTMP[0:1, 0, :], BT_PENDING.partition_broadcast(1), writes=btk)
    P.add("dve", lambda e: e.tensor_copy(out=RBflat[0:1, :], in_=BTMP[0:1, 0, :]),
          reads=btk + [("c", "rb")], writes=[("c", "rb")])
    P.add("dve", lambda e: e.tensor_copy(out=BTMP[0:1, 1, :], in_=RBflat[0:1, :]),
          reads=[("c", "rb")], writes=btk)
    P.add("dve", lambda e: e.tensor_tensor(out=LOB[0:1, :], in0=BTMP[0:1, 0, :], in1=BTMP[0:1, 1, :],
                                           op=ALU.subtract),
          reads=btk, writes=lbk)
    dma("sp", CONST_SEM + "7", RBflat[1:2, :], LOB[0:1, :], reads=lbk, writes=[("c", "rb")])

    for l in range(2):
        if l not in layers:
            continue
        for g in range(8):
            xi, xk = xin(g % 2)
            dma("sp", "xin%d" % (g % 2), xi[:, 0:128], a_w_sp_d[l, g, :, :], writes=xk)
            pst, psk = bank()
            P.add("pe", lambda e, pst=pst, xi=xi: e.transpose(out=pst[:, 0:128], in_=xi[:, 0:128],
                                                             identity=IDENT[:, :]),
                  reads=xk + [("c", "ident")], writes=[psk])
            P.add("dve", lambda e, pst=pst, l=l, g=g: e.tensor_tensor(
                out=WST[:, l, g, :], in0=pst[:, 0:128], in1=UTRI[:, :], op=ALU.mult),
                reads=[psk, ("c", "utri")], writes=[("c", "wst")])

    pieces = []

    def kview(k):
        return lambda sl: RING[:, sl, :].rearrange("p (k n) -> p k n", k=k)

    for T in range(n_tiles):
        for l in layers:
            if l < 2:
                w = a_w_in_d[l].rearrange("(k p) n -> p k n", p=128)
                for ub in range(4):
                    pieces.append((("u", T, l, ub), [(kview(8), w[:, :, ub * 512:(ub + 1) * 512])]))
                for vb in range(4):
                    pieces.append((("v", T, l, vb), [(kview(8), w[:, :, 2048 + vb * 512:2048 + (vb + 1) * 512])]))
                wo = a_w_out_d[l].rearrange("(k p) n -> p k n", p=128)
                for ob in range(4):
                    pieces.append((("ao", T, l, ob), [(kview(16), wo[:, :, ob * 256:(ob + 1) * 256])]))
            else:
                j = l - 2
                if l == 2:
                    wk = w_k_d.rearrange("(k p) (kv d) -> p k kv d", p=128, d=64)
                    wv = w_v_d.rearrange("(k p) (kv d) -> p k kv d", p=128, d=64)
                    lst = []
                    for dup in range(2):
                        for kv in range(4):
                            lst.append((lambda sl, dup=dup, kv=kv: RING[:, sl, :].rearrange(
                                "p (k kv u d) -> p k kv u d", k=8, kv=4, u=2)[:, :, kv, dup, :],
                                wk[:, :, kv, :]))
                    pieces.append((("wk", T, l, 0), lst))
                    pieces.append((("wv", T, l, 0), [
                        (lambda sl: RING[:, sl, :].rearrange("p (k n) -> p k n", k=8)[:, :, 0:256],
                         w_v_d.rearrange("(k p) n -> p k n", p=128))]))
                wq = b_w_q_d[j].rearrange("(k p) n -> p k n", p=128)
                for qb in range(2):
                    pieces.append((("q", T, l, qb), [(kview(8), wq[:, :, qb * 512:(qb + 1) * 512])]))
                wo = b_w_o_d[j].rearrange("(k p) n -> p k n", p=128)
                for ob in range(2):
                    pieces.append((("bo", T, l, ob), [(kview(8), wo[:, :, ob * 512:(ob + 1) * 512])]))
            w1 = ffn_w1_d[l].rearrange("(k p) n -> p k n", p=128)
            for hf in range(2):
                for fb in range(4):
                    c0 = (hf * 4 + fb) * 512
                    pieces.append((("w1", T, l, hf * 4 + fb), [(kview(8), w1[:, :, c0:c0 + 512])]))
                w2 = ffn_w2_d[l][hf * 2048:(hf + 1) * 2048, :].rearrange("(k p) n -> p k n", p=128)
                for ob in range(4):
                    pieces.append((("w2", T, l, hf * 4 + ob), [(kview(16), w2[:, :, ob * 256:(ob + 1) * 256])]))

    st_w = {"next_load": 0, "next_use": 0}

    def acquire(n, tags):
        lo = st_w["next_use"]
        for i in range(n):
            assert pieces[lo + i][0][0] == tags[i][0] and pieces[lo + i][0][2:] == tags[i][1:], \
                (pieces[lo + i][0], tags[i])
        st_w["next_use"] = lo + n
        while st_w["next_load"] < min(lo + NSLOT, len(pieces)):
            pi = st_w["next_load"]
            sl = pi % NSLOT
            for (vf, src) in pieces[pi][1]:
                dma("pool", "slot%d" % sl, vf(sl), src, writes=[("slot", sl)])
            st_w["next_load"] = pi + 1
        return [((lo + i) % NSLOT) for i in range(n)]

    def mm_group(out_ap, pairs, reads, pkey):
        def fn(pe):
            n = len(pairs)
            ins = None
            for i, (l_, r_) in enumerate(pairs):
                ins = pe.matmul(out_ap, l_, r_, start=(i == 0), stop=(i == n - 1))
            return ins
        P.add("pe", fn, reads=reads, writes=[pkey])
        P.tick()

    def sub(s):
        return slice(s * 512, (s + 1) * 512)

    norm_state = {"done": False}

    def norm_A(s):
        sqv, sqk = sq_v()
        P.add("act", lambda e, s=s: e.activation(out=sqv, in_=XT[:, :, sub(s)], func=AF.Square),
              reads=[("X", c, s) for c in range(8)], writes=sqk)

    def norm_B(s, gcols, outs, exact=False):
        sqv, sqk = sq_v()
        pst, psk = bank()
        P.add("pe", lambda pe, pst=pst: [pe.matmul(pst[:, :], ONESK[:, :], sqv[:, k, :], start=(k == 0), stop=(k == 7))
                                         for k in range(8)][-1],
              reads=sqk + [("c", "onesk")], writes=[psk])
        rs, rk = rstd_v(s % 2)
        if exact or not OPT_LNEXP:
            P.add("act", lambda e, pst=pst, rs=rs: e.activation(
                out=rs, in_=pst[:, :], func=AF.Sqrt, bias=EPS, scale=1.0), reads=[psk], writes=rk)
            P.add("dve", lambda e, rs=rs: e.reciprocal(out=rs, in_=rs), reads=rk, writes=rk)
        else:
            P.add("act", lambda e, pst=pst, rs=rs: e.activation(
                out=rs, in_=pst[:, :], func=AF.Ln, bias=EPS, scale=1.0), reads=[psk], writes=rk)
            P.add("act", lambda e, rs=rs: e.activation(out=rs, in_=rs, func=AF.Exp, scale=-0.5),
                  reads=rk, writes=rk)
        cnt = 0
        for gi, gc in enumerate(gcols):
            tens, coff, kn = outs[gi]
            for k in range(8):
                if exact or k < 5 or not OPT_HNSPLIT:
                    P.add("dve", lambda e, k=k, s=s, rs=rs, gc=gc, tens=tens, coff=coff:
                          e.scalar_tensor_tensor(out=tens[:, coff + k, sub(s)], in0=XT[:, k, sub(s)],
                                                 scalar=GT[:, gc + k:gc + k + 1], in1=rs,
                                                 op0=ALU.mult, op1=ALU.mult),
                          reads=[("X", k, s), ("c", "gt")] + rk, writes=[(kn, coff + k, s)])
                else:
                    tm, tmk = sview(5120 + (cnt % 2) * 512, 5120 + (cnt % 2 + 1) * 512)
                    cnt += 1
                    P.add("act", lambda e, k=k, s=s, gc=gc, tm=tm: e.activation(
                        out=tm, in_=XT[:, k, sub(s)], func=AF.Identity, scale=GT[:, gc + k:gc + k + 1]),
                        reads=[("X", k, s), ("c", "gt")], writes=tmk)
                    P.add("pool", lambda e, k=k, s=s, rs=rs, tm=tm, tens=tens, coff=coff: e.tensor_tensor(
                        out=tens[:, coff + k, sub(s)], in0=tm, in1=rs, op=ALU.mult),
                        reads=tmk + rk, writes=[(kn, coff + k, s)])

    def rms_norm(gcols, outs):
        if norm_state["done"]:
            norm_state["done"] = False
            return
        for s in range(NSUB):
            norm_A(s)
            norm_B(s, gcols, outs, exact=(outs[0][0] is XT))

    def norm_spec_mix(l):
        if l == 2:
            return [G_KV, G_MIX + l * 8], [(BIG, 8, "big"), (HN, 0, "hn")]
        return [G_MIX + l * 8], [(HN, 0, "hn")]

    def norm_spec_ffn(l):
        return [G_FFN + l * 8], [(HN, 0, "hn")]

    def proj_fm(tagname, T, l, nblk, blk0, in_tens, in_off, in_key, nk, evac):
        ncol = 4096 // nk // 128

        def groups(sl, b, s):
            wv_ = RING[:, sl, :].rearrange("p (k n) -> p k n", k=nk)
            for c4 in range(ncol):
                pst, psk = bank()
                mm_group(pst[:, :],
                         [(wv_[:, k, c4 * 128:(c4 + 1) * 128], in_tens[:, in_off + k, sub(s)])
                          for k in range(nk)],
                         reads=[("slot", sl)] + [(in_key, in_off + k, s) for k in range(nk)], pkey=psk)
                evac(pst, psk, (blk0 + b) * ncol + c4, s)

        sls_ = acquire(2, [(tagname, l, blk0), (tagname, l, blk0 + 1)])
        for s in range(NSUB):
            for bi in range(2):
                groups(sls_[bi], bi, s)
        for b in range(2, nblk):
            (sl,) = acquire(1, [(tagname, l, blk0 + b)])
            for s in range(NSUB):
                groups(sl, b, s)

    def proj_fm_souter(tagname, T, l, nblk, blk0, in_tens, in_off, in_key, nk, evac, next_norm=None,
                       pre_hook=None):
        sls_ = acquire(nblk, [(tagname, l, blk0 + b) for b in range(nblk)])
        ncol = 4096 // nk // 128
        ngrp = [0]
        for s in range(NSUB):
            for b in range(nblk):
                wv_ = RING[:, sls_[b], :].rearrange("p (k n) -> p k n", k=nk)
                for c4 in range(ncol):
                    pst, psk = bank()
                    mm_group(pst[:, :],
                             [(wv_[:, k, c4 * 128:(c4 + 1) * 128], in_tens[:, in_off + k, sub(s)])
                              for k in range(nk)],
                             reads=[("slot", sls_[b])] + [(in_key, in_off + k, s) for k in range(nk)], pkey=psk)
                    evac(pst, psk, (blk0 + b) * ncol + c4, s)
                    if next_norm is not None:
                        cq = (blk0 + b) * ncol + c4 - (blk0 * ncol)
                        sqv_, _ = sq_v()
                        _, sqkc = sview(2048 + cq * 256, 2048 + (cq + 1) * 256)
                        P.add("act", lambda e, cq=cq, s=s, sqv_=sqv_: e.activation(
                            out=sqv_[:, cq, :], in_=XT[:, cq, sub(s)], func=AF.Square),
                            reads=[("X", cq, s)], writes=sqkc)
                    ngrp[0] += 1
                    if pre_hook is not None and ngrp[0] == pre_hook[0]:
                        pre_hook[1]()
            if next_norm is not None:
                gcols, outs = next_norm
                P.defer(1 if s == 0 else 2,
                        lambda s=s, gcols=gcols, outs=outs: norm_B(s, gcols, outs, exact=(outs[0][0] is XT)))
        if next_norm is not None:
            norm_state["done"] = True

    def evac_resid(pst, psk, c, s):
        P.add("dve", lambda e: e.tensor_tensor(out=XT[:, c, sub(s)], in0=XT[:, c, sub(s)],
                                               in1=pst[:, :], op=ALU.add),
              reads=[psk, ("X", c, s)], writes=[("X", c, s)])

    def ffn(T, l, nxt=None):
        rms_norm([G_FFN + l * 8], [(HN, 0, "hn")])
        for hf in range(2):
            def evac_relu2(pst, psk, c, s, hf=hf):
                cc = c - hf * 16
                rv, rk = relu_v((cc + s) % 2)
                P.add("act", lambda e: e.activation(out=rv, in_=pst[:, :], func=AF.Relu),
                      reads=[psk], writes=rk)
                P.add("pool", lambda e: e.tensor_tensor(out=BIG[:, cc, sub(s)], in0=rv, in1=rv, op=ALU.mult),
                      reads=rk, writes=[("big", cc, s)])
            proj_fm("w1", T, l, 4, hf * 4, HN, 0, "hn", 8, evac_relu2)

            def evac_r2(pst, psk, c, s, hf=hf):
                evac_resid(pst, psk, c - hf * 8, s)
            if hf == 1:
                proj_fm_souter("w2", T, l, 4, hf * 4, BIG, 0, "big", 16, evac_r2, next_norm=nxt)
            else:
                proj_fm("w2", T, l, 4, hf * 4, BIG, 0, "big", 16, evac_r2)

    def mixer_a(T, l):
        rms_norm([G_MIX + l * 8], [(HN, 0, "hn")])
        dma("sp", "lng", LNG, a_ln_g_d[l:l + 1, :].partition_broadcast(128)[:, 0],
            writes=[("KT", 0), ("KT", 1)])

        def evac_gelu(pst, psk, c, s):
            P.add("act", lambda e: e.activation(out=BIG[:, c, sub(s)], in_=pst[:, :], func=AF.Gelu),
                  reads=[psk], writes=[("big", c, s)])
        proj_fm("u", T, l, 4, 0, HN, 0, "hn", 8, evac_gelu)

        sls = acquire(4, [("v", l, vb) for vb in range(4)])

        def v_stage(t):
            s = t // 4
            tsl = slice(t * 128, (t + 1) * 128)
            vt, vk = v_t(t % 2)
            vn, vnk = vn_t(t % 2)
            skey = ("stat", t % 2)
            mvk = ("mv", t % 2)
            for vb in range(4):
                wv_ = RING[:, sls[vb], :].rearrange("p (k n) -> p k n", k=8)
                pst, psk = bank()
                mm_group(pst[:, :], [(HN[:, k, tsl], wv_[:, k, :]) for k in range(8)],
                         reads=[("slot", sls[vb])] + [("hn", k, s) for k in range(8)], pkey=psk)
                P.add("act", lambda e, pst=pst, vt=vt, vb=vb: e.activation(
                    out=vt[:, vb * 512:(vb + 1) * 512], in_=pst[:, :], func=AF.Gelu),
                    reads=[psk], writes=vk[vb * 2:(vb + 1) * 2])
                P.add("dve", lambda e, vt=vt, vb=vb, t=t: e.bn_stats(
                    out=STAT[:, t % 2, vb, :], in_=vt[:, vb * 512:(vb + 1) * 512]),
                    reads=vk[vb * 2:(vb + 1) * 2], writes=[skey])
            P.add("dve", lambda e, t=t: e.bn_aggr(out=MV[:, t % 2, 0:2],
                                                  in_=STAT[:, t % 2, :, :].rearrange("p a b -> p (a b)")),
                  reads=[skey], writes=[mvk])
            P.add("act", lambda e, t=t: e.activation(out=MV[:, t % 2, 1:2], in_=MV[:, t % 2, 1:2],
                                                     func=AF.Sqrt, bias=EPS, scale=1.0),
                  reads=[mvk], writes=[mvk])
            P.add("dve", lambda e, t=t: e.reciprocal(out=MV[:, t % 2, 1:2], in_=MV[:, t % 2, 1:2]),
                  reads=[mvk], writes=[mvk])
            P.add("dve", lambda e, t=t: e.scalar_tensor_tensor(
                out=MV[:, t % 2, 2:3], in0=MV[:, t % 2, 0:1], scalar=-1.0, in1=MV[:, t % 2, 1:2],
                op0=ALU.mult, op1=ALU.mult), reads=[mvk], writes=[mvk])
            for half, eng in ((0, "pool"), (1, "dve" if OPT_LNHALF else "pool")):
                hs = slice(half * 1024, (half + 1) * 1024)
                vkh = vk[half * 4:(half + 1) * 4]
                vnkh = vnk[half * 2:(half + 1) * 2]
                P.add("act", lambda e, t=t, vt=vt, hs=hs: e.activation(
                    out=vt[:, hs], in_=vt[:, hs], func=AF.Identity, bias=MV[:, t % 2, 2:3], scale=MV[:, t % 2, 1:2]),
                    reads=[mvk] + vkh, writes=vkh)
                P.add(eng, lambda e, vt=vt, vn=vn, hs=hs: e.tensor_tensor(out=vn[:, hs], in0=vt[:, hs], in1=LNG[:, hs],
                                                                         op=ALU.mult),
                      reads=vkh + [("KT", 0), ("KT", 1)], writes=vnkh)

        def sp_stage(t):
            s = t // 4
            tsl = slice(t * 128, (t + 1) * 128)
            vn, vnk = vn_t(t % 2)
            for c0 in range(0, 16, 4):
                pst, psk = bank()

                def fn(pe, pst=pst, c0=c0, vn=vn):
                    ins = None
                    for ci in range(4):
                        c = c0 + ci
                        g = c // 2
                        o_ = pst[:, ci * 128:(ci + 1) * 128]
                        pe.matmul(o_, vn[:, c * 128:(c + 1) * 128], WST[:, l, g, :], start=True, stop=False)
                        ins = pe.matmul(o_, E2[:, :], RB[:, l, g, :], start=False, stop=True)
                    return ins
                P.add("pe", fn, reads=vnk + [("c", "wst"), ("c", "rb"), ("c", "e2")], writes=[psk])
                P.add("dve", lambda e, pst=pst, c0=c0, tsl=tsl: e.tensor_tensor(
                    out=BIG[:, c0:c0 + 4, tsl], in0=pst[:, :].rearrange("p (a b) -> p a b", a=4),
                    in1=BIG[:, c0:c0 + 4, tsl], op=ALU.mult),
                    reads=[psk] + [("big", c0 + ci, s) for ci in range(4)],
                    writes=[("big", c0 + ci, s) for ci in range(4)])

        v_stage(0)
        for t in range(NCH - 1):
            v_stage(t + 1)
            sp_stage(t)
        proj_fm_souter("ao", T, l, 4, 0, BIG, 0, "big", 16, evac_resid, next_norm=norm_spec_ffn(l),
                       pre_hook=(4, lambda: sp_stage(NCH - 1)))

    def mixer_b(T, l):
        j = l - 2
        if l == 2:
            rms_norm(*norm_spec_mix(2))
            (slk,) = acquire(1, [("wk", l, 0)])
            wkv = RING[:, slk, :].rearrange("p (k n) -> p k n", k=8)
            for s in range(NSUB):
                for kv in range(4):
                    pst, psk = bank()
                    mm_group(pst[:, :], [(wkv[:, k, kv * 128:(kv + 1) * 128], BIG[:, 8 + k, sub(s)])
                                         for k in range(8)],
                             reads=[("slot", slk)] + [("big", 8 + k, s) for k in range(8)], pkey=psk)
                    P.add("act", lambda e, pst=pst, kv=kv, s=s: e.activation(
                        out=KTC[:, kv, sub(s)], in_=pst[:, :], func=AF.Copy),
                        reads=[psk], writes=[("KT", s)])
            (slv,) = acquire(1, [("wv", l, 0)])
            wvv = RING[:, slv, :].rearrange("p (k n) -> p k n", k=8)
            for t in range(NCH):
                s = t // 4
                pst, psk = bank()
                mm_group(pst[:, 0:256], [(BIG[:, 8 + k, t * 128:(t + 1) * 128], wvv[:, k, 0:256]) for k in range(8)],
                         reads=[("slot", slv)] + [("big", 8 + k, s) for k in range(8)], pkey=psk)
                P.add("act", lambda e, pst=pst, t=t: e.activation(
                    out=VC[:, t, :].rearrange("p (g c) -> p g c", g=4)[:, :, 64:128],
                    in_=pst[:, 0:256].rearrange("p (g c) -> p g c", g=4), func=AF.Copy),
                    reads=[psk], writes=[("V", s)])
        else:
            rms_norm([G_MIX + l * 8], [(HN, 0, "hn")])

        def evac_q(pst, psk, c, s):
            P.add("act", lambda e: e.mul(out=BIG[:, c, sub(s)], in_=pst[:, :], mul=0.125),
                  reads=[psk], writes=[("big", c, s)])
        proj_fm("q", T, l, 2, 0, HN, 0, "hn", 8, evac_q)

        its = [(t, kvg) for t in range(NCH) for kvg in range(4)]

        def jcs_of(t):
            return [1] if (T * NCH + t) == 0 else [0, 1]

        def e32_rng(n, par, lo):
            base = (0, 3072, 4096)[n % 3] + par * 512
            return sview(base + lo, base + 512)

        def eb_rng(n, par, lo):
            base = (1024, 1536, 2560)[n % 3] + par * 256
            a, k = sview(base, base + 256)
            return a.bitcast(BF16), k

        def stage_qk(n):
            t, kvg = its[n]
            s = t // 4
            tsl = slice(t * 128, (t + 1) * 128)
            jcs = jcs_of(t)
            lo = 0 if 0 in jcs else 256
            for par in range(2):
                psl = slice(par * 64, (par + 1) * 64)
                pst, psk = bank()

                def fn(pe, pst=pst, psl=psl, kvg=kvg, t=t, tsl=tsl, jcs=jcs):
                    ins = None
                    for jc in jcs:
                        if jc == 1:
                            kt = KTC[psl, kvg, tsl]
                        elif t == 0:
                            kt = KTP[psl, kvg, :]
                        else:
                            kt = KTC[psl, kvg, (t - 1) * 128:t * 128]
                        o_ = pst[:, jc * 256:(jc + 1) * 256].rearrange("p (d e) -> p d e", d=2)
                        ins = pe.matmul(o_, kt, BIG[psl, 2 * kvg:2 * kvg + 2, tsl], start=True, stop=True)
                    return ins
                rds = [("big", 2 * kvg, s), ("big", 2 * kvg + 1, s), ("KT", s)]
                if 0 in jcs:
                    rds.append(("KTp",) if t == 0 else ("KT", (t - 1) // 4))
                P.add("pe", fn, reads=rds, writes=[psk])
                ea, ek = e32_rng(n, par, lo)
                P.add("act", lambda e, pst=pst, lo=lo, ea=ea: e.activation(
                    out=ea, in_=pst[:, lo:512], func=AF.Exp), reads=[psk], writes=ek)

        def stage_mul(n):
            t, kvg = its[n]
            lo = 0 if 0 in jcs_of(t) else 256
            for par, eng in ((0, "pool"), (1, "pool")):
                ea, ek = e32_rng(n, par, lo)
                ba, bk = eb_rng(n, par, 0)
                P.add(eng, lambda e, ea=ea, ba=ba, kvg=kvg, par=par, lo=lo: e.tensor_tensor(
                    out=ba[:, lo:512], in0=ea, in1=EXPB[:, kvg * 2 + par, lo:512], op=ALU.mult),
                    reads=ek + [("c", "expb")], writes=bk)

        def stage_pv(n):
            t, kvg = its[n]
            s = t // 4
            tsl = slice(t * 128, (t + 1) * 128)
            jcs = jcs_of(t)
            ebs = [eb_rng(n, par, 0) for par in range(2)]
            NDb, NDk = bank()

            def fnpv(pe, kvg=kvg, t=t, ebs=ebs, NDb=NDb, jcs=jcs):
                ins = None
                for use_v in (True, False):
                    o_ = NDb[:, (0 if use_v else 256):(256 if use_v else 512)].rearrange("p (d e) -> p d e", d=2)
                    mms = [(par, jc) for par in range(2) for jc in jcs]
                    for mi, (par, jc) in enumerate(mms):
                        ev = ebs[par][0].rearrange("p (c d e) -> p c d e", c=2, d=2)
                        c0 = 64 if par == 0 else 0
                        if not use_v:
                            lhs = HONES[:, c0:c0 + 128]
                        elif jc == 1:
                            lhs = VC[:, t, kvg * 192 + c0:kvg * 192 + c0 + 128]
                        elif t == 0:
                            lhs = VP[:, kvg * 192 + c0:kvg * 192 + c0 + 128]
                        else:
                            lhs = VC[:, t - 1, kvg * 192 + c0:kvg * 192 + c0 + 128]
                        ins = pe.matmul(o_, lhs, ev[:, jc, :, :], start=(mi == 0), stop=(mi == len(mms) - 1))
                return ins
            rds = ebs[0][1] + ebs[1][1] + [("V", s), ("c", "hones")]
            if 0 in jcs:
                rds.append(("Vp",) if t == 0 else ("V", (t - 1) // 4))
            P.add("pe", fnpv, reads=rds, writes=[NDk])
            rd, rdk = rden_v(n % 2)
            P.add("dve", lambda e, NDb=NDb, rd=rd, kvg=kvg: e.tensor_scalar(
                out=rd[:, 0:128], in0=NDb[:, 256:384], scalar1=ESC[:, j, kvg, 0:1], scalar2=None, op0=ALU.add),
                reads=[NDk, ("c", "esc")], writes=[rdk[0], ("rdh", n % 2, 0)])
            P.add("dve", lambda e, rd=rd: e.reciprocal(out=rd[:, 0:128], in_=rd[:, 0:128]),
                  reads=[("rdh", n % 2, 0)], writes=[("rdh", n % 2, 0)])
            P.add("act", lambda e, NDb=NDb, rd=rd, kvg=kvg: e.activation(
                out=rd[:, 128:256], in_=NDb[:, 384:512], func=AF.Ln, bias=ESC[:, j, kvg, 1:2], scale=1.0),
                reads=[NDk, ("c", "esc")], writes=[rdk[0], ("rdh", n % 2, 1)])
            P.add("act", lambda e, rd=rd: e.activation(out=rd[:, 128:256], in_=rd[:, 128:256], func=AF.Exp, scale=-1.0),
                  reads=[("rdh", n % 2, 1)], writes=[("rdh", n % 2, 1)])
            P.add("dve", lambda e, NDb=NDb, rd=rd, kvg=kvg, tsl=tsl: e.tensor_tensor(
                out=BIG[:, 8 + 2 * kvg:8 + 2 * kvg + 2, tsl],
                in0=NDb[:, 0:256].rearrange("p (d e) -> p d e", d=2),
                in1=rd.rearrange("p (d e) -> p d e", d=2), op=ALU.mult),
                reads=[NDk, ("rdh", n % 2, 0), ("rdh", n % 2, 1)],
                writes=[("big", 8 + 2 * kvg, s), ("big", 8 + 2 * kvg + 1, s)])

        for n0 in range(OPT_DEPTH):
            stage_qk(n0)
        for n in range(len(its)):
            if n + OPT_DEPTH < len(its):
                stage_qk(n + OPT_DEPTH)
            stage_mul(n)
            stage_pv(n)
        if debug == "attn":
            dbg_d = nc.dram_tensor("dbg", [128, 16 * TT], BF16, kind="ExternalOutput").ap()
            dma("sp", "dbg", dbg_d[:, :], BIG[:, :, :].rearrange("p a b -> p (a b)"),
                reads=[("big", c, s) for c in range(16) for s in range(NSUB)])
            raise StopIteration
        proj_fm_souter("bo", T, l, 2, 0, BIG, 8, "big", 8, evac_resid, next_norm=norm_spec_ffn(l))
        if l == 3 or (3 not in layers):
            P.add("pool", lambda e: e.tensor_copy(out=KTP[:, :, :], in_=KTC[:, :, TT - 128:TT]),
                  reads=[("KT", 1)], writes=[("KTp",)])
            P.add("pool", lambda e: e.tensor_copy(out=VP[:, :], in_=VC[:, NCH - 1, :]),
                  reads=[("V", 1)], writes=[("Vp",)])

    for T in range(n_tiles):
        tok0 = T * TT
        for t in range(NCH):
            xi, xk = xin(t % 4)
            dma("sp", "xin%d" % (t % 4), xi, x_d[tok0 + t * 128:tok0 + (t + 1) * 128, :], writes=xk)
            for cg in range(2):
                pst, psk = bank()

                def fn(pe, pst=pst, xi=xi, cg=cg):
                    ins = None
                    for ci in range(4):
                        c = cg * 4 + ci
                        ins = pe.transpose(out=pst[:, ci * 128:(ci + 1) * 128], in_=xi[:, c * 128:(c + 1) * 128],
                                           identity=IDENT[:, :])
                    return ins
                P.add("pe", fn, reads=xk + [("c", "ident")], writes=[psk])
                P.add("act", lambda e, pst=pst, cg=cg, t=t: e.activation(
                    out=XT[:, cg * 4:(cg + 1) * 4, t * 128:(t + 1) * 128],
                    in_=pst[:, :].rearrange("p (a b) -> p a b", a=4), func=AF.Copy),
                    reads=[psk], writes=[("X", cg * 4 + ci, t // 4) for ci in range(4)])
        try:
            for l in layers:
                if l < 2:
                    mixer_a(T, l)
                else:
                    mixer_b(T, l)
                if debug == "noffn":
                    for _ in range(16):
                        acquire(1, [pieces[st_w["next_use"]][0][0:1] + pieces[st_w["next_use"]][0][2:]])
                    continue
                li = layers.index(l)
                if li + 1 < len(layers):
                    nxt = norm_spec_mix(layers[li + 1])
                elif final_norm:
                    nxt = ([G_FIN], [(XT, 0, "X")])
                else:
                    nxt = None
                ffn(T, l, nxt)
        except StopIteration:
            st_w["next_use"] = len(pieces)
            break
        if final_norm:
            rms_norm([G_FIN], [(XT, 0, "X")])
        P.flush()
        for t in range(NCH):
            yo, yk = xin(t % 4)
            for cg in range(2):
                pst, psk = bank()

                def fn(pe, pst=pst, cg=cg, t=t):
                    ins = None
                    for ci in range(4):
                        c = cg * 4 + ci
                        ins = pe.transpose(out=pst[:, ci * 128:(ci + 1) * 128],
                                           in_=XT[:, c, t * 128:(t + 1) * 128], identity=IDENT[:, :])
                    return ins
                P.add("pe", fn, reads=[("X", cg * 4 + ci, t // 4) for ci in range(4)] + [("c", "ident")],
                      writes=[psk])
                P.add("act", lambda e, pst=pst, cg=cg, yo=yo: e.activation(
                    out=yo[:, cg * 512:(cg + 1) * 512], in_=pst[:, :], func=AF.Copy),
                    reads=[psk], writes=yk)
            dma("sp", "xin%d" % (t % 4), y_d[tok0 + t * 128:tok0 + (t + 1) * 128, :], yo, reads=yk)

    assert st_w["next_use"] == len(pieces), (st_w, len(pieces))

    P.finalize()
    keys = list(P.ENG) + sorted(P.dma_cnt.keys())
    with ExitStack() as es:
        sems = {k: es.enter_context(nc.semaphore("s_" + k)) for k in keys}
        block = es.enter_context(nc.Block())

        @block.tensor
        def _(e):
            P.emit("pe", e, sems)

        @block.scalar
        def _(e):
            P.emit("act", e, sems)

        @block.vector
        def _(e):
            P.emit("dve", e, sems)

        @block.gpsimd
        def _(e):
            P.emit("pool", e, sems)

        @block.sync
        def _(e):
            P.emit("sp", e, sems, final_waits=("xin0", "xin1", "xin2", "xin3", "dbg"))
    return nc


def host_consts(mix_norm_g, ffn_norm_g, kv_norm_g, final_norm_g, rel_bias):
    f = np.float32
    gstack = np.concatenate([np.asarray(mix_norm_g, f).reshape(32, 128),
                             np.asarray(ffn_norm_g, f).reshape(32, 128),
                             np.asarray(kv_norm_g, f).reshape(8, 128),
                             np.asarray(final_norm_g, f).reshape(8, 128)], axis=0)
    bucket, inwin = _bucket_table()
    rb = np.asarray(rel_bias, f)
    bias_full = np.transpose(rb[bucket], (2, 0, 1))
    bf = bias_full.reshape(4, 2, 2, 128, 2, 128)
    biasg = np.ascontiguousarray(np.transpose(bf, (5, 0, 2, 4, 1, 3))).reshape(128, 16 * 256)
    amask = np.ascontiguousarray(
        np.transpose(inwin.reshape(128, 2, 128), (2, 1, 0))).astype(f).reshape(128, 256)
    ident = np.eye(128, dtype=f)
    utri = np.triu(np.ones((128, 128), f))
    return dict(gstack=np.ascontiguousarray(gstack), biasg=biasg, amask=amask, ident=ident, utri=utri)


_NC_CACHE = {}


def kernel(x, mix_norm_g, ffn_norm_g, a_w_in, a_ln_g, a_w_spatial, a_b_spatial, a_w_out,
           kv_norm_g, w_k, w_v, b_w_q, b_sinks, b_w_o, rel_bias, ffn_w1, ffn_w2, final_norm_g):
    f = np.float32
    x = np.asarray(x, f)
    B = x.shape[0]
    if "nc" not in _NC_CACHE:
        _NC_CACHE["nc"] = build_nc()
    nc = _NC_CACHE["nc"]
    common = host_consts(mix_norm_g, ffn_norm_g, kv_norm_g, final_norm_g, rel_bias)
    for k, v in dict(a_w_in=a_w_in, a_ln_g=a_ln_g, a_w_spatial=a_w_spatial, a_b_spatial=a_b_spatial,
                     a_w_out=a_w_out, w_k=w_k, w_v=w_v, b_w_q=b_w_q, b_sinks=b_sinks, b_w_o=b_w_o,
                     ffn_w1=ffn_w1, ffn_w2=ffn_w2).items():
        common[k] = np.ascontiguousarray(np.asarray(v, f))
    in_maps = []
    for b in range(B):
        m = dict(common)
        m["x"] = np.ascontiguousarray(x[b])
        in_maps.append(m)
    res = run_bass_kernel_spmd(nc, in_maps, core_ids=list(range(B)))
    return np.stack([np.asarray(r["y"], f) for r in res.results], axis=0)
```

```python
import math
import os
from contextlib import ExitStack

import numpy as np
import concourse.bass as bass
import concourse.mybir as mybir
from concourse.bass_utils import run_bass_kernel_spmd

F32 = mybir.dt.float32
BF16 = mybir.dt.bfloat16
AF = mybir.ActivationFunctionType
ALU = mybir.AluOpType

SEQ = 4096
D = 1024
TT = 1024
NSUB = TT // 512
NCH = TT // 128
NSLOT = 6
EPS = 1e-6
OPT_HNSPLIT = int(os.environ.get("K_HNSPLIT", "0"))
OPT_LNEXP = int(os.environ.get("K_LNEXP", "1"))
OPT_DEPTH = int(os.environ.get("K_DEPTH", "2"))
OPT_LNHALF = int(os.environ.get("K_LNHALF", "0"))


class Op:
    __slots__ = ("eng", "fn", "dma", "deps", "signal", "done_key", "done_val")


class Prog:
    ENG = ("pe", "act", "dve", "pool", "sp")

    def __init__(self):
        self.eng_ops = {e: [] for e in self.ENG}
        self.res = {}
        self.dma_cnt = {}
        self.deferred = []

    def add(self, eng, fn, reads=(), writes=(), dma=None):
        o = Op()
        o.eng, o.fn, o.dma, o.signal = eng, fn, dma, False
        if dma is not None:
            c = self.dma_cnt.get(dma, 0) + 1
            self.dma_cnt[dma] = c
            o.done_key, o.done_val = dma, 16 * c
        else:
            o.done_key, o.done_val = eng, None
        deps = {}

        def need(d, raw, waw=False, strict=False):
            if d is o:
                return
            if waw and dma is not None and d.dma == dma:
                return
            if d.dma is None and d.eng == eng and dma is None:
                if eng == "pe" or not (raw or strict):
                    return
            deps[id(d)] = d

        res = self.res
        for r in reads:
            st = res.get(r)
            if st is None:
                st = res[r] = [None, {}]
            if st[0] is not None:
                need(st[0], True)
        for w in writes:
            st = res.get(w)
            if st is None:
                st = res[w] = [None, {}]
            strict = (w[0] == "c")
            if st[0] is not None:
                need(st[0], False, True, strict)
            for rd in st[1].values():
                need(rd, False, False, strict)
        for r in reads:
            res[r][1][o.done_key] = o
        for w in writes:
            st = res[w]
            st[0] = o
            st[1] = {}
        o.deps = list(deps.values())
        for d in o.deps:
            d.signal = True
        self.eng_ops[eng].append(o)
        return o

    def defer(self, n, fn):
        self.deferred.append([n, fn])

    def tick(self):
        due = [d for d in self.deferred if d[0] <= 1]
        self.deferred = [[d[0] - 1, d[1]] for d in self.deferred if d[0] > 1]
        for d in due:
            d[1]()

    def flush(self):
        due, self.deferred = self.deferred, []
        for d in due:
            d[1]()

    def finalize(self):
        for e in self.ENG:
            c = 0
            for o in self.eng_ops[e]:
                if o.dma is None and o.signal:
                    c += 1
                    o.done_val = c

    def emit(self, ename, eng, sems, final_waits=()):
        waited = {}
        for o in self.eng_ops[ename]:
            for d in o.deps:
                k, v = d.done_key, d.done_val
                if waited.get(k, 0) < v:
                    eng.wait_ge(sems[k], v)
                    waited[k] = v
            ins = o.fn(eng)
            if o.dma is not None:
                ins.then_inc(sems[o.dma], 16)
            elif o.signal:
                ins.then_inc(sems[ename], 1)
        for k in final_waits:
            v = 16 * self.dma_cnt.get(k, 0)
            if v:
                eng.wait_ge(sems[k], v)


def _bucket_table():
    qi = np.arange(128)[:, None]
    kj = np.arange(256)[None, :]
    dist = qi + 128 - kj
    d = np.maximum(dist, 0)
    lr = (np.log(np.maximum(d, 1).astype(np.float32) / np.float32(16)) /
          np.float32(math.log(128 / 16))).astype(np.float32)
    large = np.minimum(16 + (lr * np.float32(16)).astype(np.int32), 31)
    bucket = np.where(d < 16, d, large)
    inwin = (dist >= 0) & (dist < 128)
    return bucket, inwin


def build_nc(n_tiles=4, layers=(0, 1, 2, 3), final_norm=True, debug=None):
    nc = bass.Bass("TRN2", target_bir_lowering=False)
    P = Prog()

    def din(name, shape):
        return nc.dram_tensor(name, list(shape), F32, kind="ExternalInput").ap()

    x_d = din("x", (SEQ, D))
    gstack_d = din("gstack", (80, 128))
    a_w_in_d = din("a_w_in", (2, 1024, 4096))
    a_ln_g_d = din("a_ln_g", (2, 2048))
    a_w_sp_d = din("a_w_spatial", (2, 8, 128, 128))
    a_b_sp_d = din("a_b_spatial", (2, 8, 128))
    a_w_out_d = din("a_w_out", (2, 2048, 1024))
    w_k_d = din("w_k", (1024, 256))
    w_v_d = din("w_v", (1024, 256))
    b_w_q_d = din("b_w_q", (2, 1024, 1024))
    b_sinks_d = din("b_sinks", (2, 16))
    b_w_o_d = din("b_w_o", (2, 1024, 1024))
    biasg_d = din("biasg", (128, 16 * 256))
    ffn_w1_d = din("ffn_w1", (4, 1024, 4096))
    ffn_w2_d = din("ffn_w2", (4, 4096, 1024))
    ident_d = din("ident", (128, 128))
    utri_d = din("utri", (128, 128))
    amask_d = din("amask", (128, 256))
    y_d = nc.dram_tensor("y", [SEQ, D], F32, kind="ExternalOutput").ap()

    def sb(name, shape, dt):
        return nc.alloc_sbuf_tensor(name, list(shape), dt)

    XT = sb("XT", [128, 8, TT], F32)
    HN = sb("HN", [128, 8, TT], BF16)
    BIG = sb("BIG", [128, 16, TT], BF16)
    RING = sb("RING", [128, NSLOT, 4096], BF16)
    S = sb("S", [128, 6144], F32)
    KTC = sb("KTC", [128, 4, TT], BF16)
    VC = sb("VC", [128, 8, 768], BF16)
    KTP = sb("KTP", [128, 4, 128], BF16)
    VP = sb("VP", [128, 768], BF16)
    EXPB = sb("EXPB", [128, 8, 512], F32)
    RB = sb("RB", [128, 2, 8, 128], BF16)
    WST = sb("WST", [128, 2, 8, 128], BF16)
    IDENT = sb("IDENT", [128, 128], F32)
    UTRI = sb("UTRI", [128, 128], F32)
    ONES = sb("ONES", [128, 128], BF16)
    ONESK = sb("ONESK", [128, 128], BF16)
    E2 = sb("E2", [128, 128], BF16)
    GT = sb("GT", [128, 80], F32)
    ES = sb("ES", [128, 2, 16], F32)
    ESC = sb("ESC", [128, 2, 4, 2], F32)
    HONES = sb("HONES", [128, 192], BF16)
    STAT = sb("STAT", [128, 2, 4, 6], F32)
    MV = sb("MV", [128, 2, 4], F32)
    GST = sb("GST", [80, 128], F32)

    LNG = KTC[:, :, :].rearrange("p a b -> p (a b)").bitcast(F32)

    PS = [nc.alloc_psum_tensor("ps%d" % i, [128, 512], F32) for i in range(8)]
    ps_rr = [0]

    def bank():
        i = ps_rr[0]
        ps_rr[0] = (i + 1) % 8
        return PS[i], ("ps", i)

    def sview(lo, hi):
        return S[:, lo:hi], [("S", b) for b in range(lo // 256, (hi - 1) // 256 + 1)]

    def xin(i):
        return sview(i * 1024, (i + 1) * 1024)

    def v_t(i):
        return sview(i * 2048, (i + 1) * 2048)

    def vn_t(i):
        a, k = sview(4096 + i * 1024, 4096 + (i + 1) * 1024)
        return a.bitcast(BF16), k

    def sq_v():
        a, k = sview(2048, 4096)
        return a.bitcast(BF16).rearrange("p (k n) -> p k n", k=8), k

    def rstd_v(i):
        return sview(4096 + i * 512, 4096 + (i + 1) * 512)

    def relu_v(i):
        return sview(i * 512, (i + 1) * 512)

    def e32_v():
        return sview(0, 1024)

    def e_v(i):
        a, k = sview(1024 + i * 512, 1024 + (i + 1) * 512)
        return a.bitcast(BF16), k

    def rden_v(i):
        return sview(2048 + i * 256, 2048 + (i + 1) * 256)

    def dma(eng, key, out, in_, reads=(), writes=()):
        P.add(eng, lambda e, out=out, in_=in_: e.dma_start(out=out, in_=in_),
              reads=reads, writes=writes, dma=key)

    CONST_SEM = "cst"
    dma("sp", CONST_SEM, IDENT[:, :], ident_d[:, :], writes=[("c", "ident")])
    dma("sp", CONST_SEM + "1", UTRI[:, :], utri_d[:, :], writes=[("c", "utri")])
    dma("sp", CONST_SEM + "2", GST[:, :], gstack_d[:, :], writes=[("c", "gst")])
    dma("sp", CONST_SEM + "3", EXPB[:, :, :].rearrange("p a b -> p (a b)"), biasg_d[:, :],
        writes=[("c", "expb")])
    am_ap, am_k = sview(0, 256)
    dma("sp", CONST_SEM + "4", am_ap, amask_d[:, :], writes=am_k)
    dma("sp", CONST_SEM + "5", ES[:, :, :].rearrange("p a b -> p (a b)"),
        b_sinks_d[:, :].rearrange("l h -> (l h)").partition_broadcast(128), writes=[("c", "es")])
    bsrc = a_b_sp_d[:, :, :].rearrange("l g i -> (l g i)")
    BT_PENDING = bsrc

    P.add("pool", lambda e: e.memset(ONES[:, :], 1.0), writes=[("c", "ones")])
    P.add("pool", lambda e: e.memset(ONESK[:, :], 1.0 / 1024.0), writes=[("c", "onesk")])
    P.add("pool", lambda e: e.memset(E2[:, :], 0.0), writes=[("c", "e2")])
    P.add("pool", lambda e: e.memset(E2[0:2, :], 1.0), writes=[("c", "e2")])
    P.add("pool", lambda e: e.memset(RB[:, :, :, :], 0.0), writes=[("c", "rb")])
    P.add("pool", lambda e: e.memset(KTP[:, :, :], 0.0), writes=[("KTp",)])
    P.add("pool", lambda e: e.memset(VP[:, :], 0.0), writes=[("Vp",)])
    P.add("pool", lambda e: e.memset(VC[:, :, :], 0.0), writes=[("V", 0), ("V", 1)])
    P.add("pool", lambda e: e.memset(HONES[:, :], 0.0), writes=[("c", "hones")])
    P.add("pool", lambda e: e.memset(HONES[:, 64:128], 1.0), writes=[("c", "hones")])

    P.add("act", lambda e: e.activation(out=ES[:, :, :], in_=ES[:, :, :], func=AF.Exp),
          reads=[("c", "es")], writes=[("c", "es")])
    ES5 = ES[:, :, :].rearrange("p l (g h r) -> p l g h r", g=4, h=2)
    for par in range(2):
        P.add("dve", lambda e, par=par: e.tensor_copy(out=ESC[par * 64:(par + 1) * 64, :, :, :],
                                                      in_=ES5[par * 64:(par + 1) * 64, :, :, :, par]),
              reads=[("c", "es")], writes=[("c", "esc")])
    P.add("act", lambda e: e.activation(out=EXPB[:, :, :], in_=EXPB[:, :, :], func=AF.Exp),
          reads=[("c", "expb")], writes=[("c", "expb")])
    for g16 in range(8):
        def fn(e, g16=g16):
            m = am_ap.rearrange("p (c e) -> p c e", c=2)
            ins = None
            for hh in range(2):
                v = EXPB[:, g16, :].rearrange("p (c d e) -> p c d e", c=2, d=2)[:, :, hh, :]
                ins = e.tensor_tensor(out=v, in0=v, in1=m, op=ALU.mult)
            return ins
        P.add("pool", fn, reads=[("c", "expb")] + am_k, writes=[("c", "expb")])

    pst, psk = bank()
    P.add("pe", lambda e, pst=pst: e.transpose(out=pst[:, 0:80], in_=GST[:, :], identity=IDENT[0:80, 0:80]),
          reads=[("c", "gst"), ("c", "ident")], writes=[psk])
    P.add("dve", lambda e, pst=pst: e.tensor_copy(out=GT[:, :], in_=pst[:, 0:80]),
          reads=[psk], writes=[("c", "gt")])
    G_MIX, G_FFN, G_KV, G_FIN = 0, 32, 64, 72

    RBflat = RB[:, :, :, :].rearrange("p a b c -> p (a b c)")
    _bt, btk = sview(2048, 6144)
    BTMP = _bt[0:1, :].rearrange("p (a b) -> p a b", a=2)
    _lb, lbk = sview(1024, 2048)
    LOB = _lb.bitcast(BF16)
    dma("sp", CONST_SEM + "6", BTMP[0:1, 0, :], BT_PENDING.partition_broadcast(1), writes=btk)
    P.add("dve", lambda e: e.tensor_copy(out=RBflat[0:1, :], in_=BTMP[0:1, 0, :]),
          reads=btk + [("c", "rb")], writes=[("c", "rb")])
    P.add("dve", lambda e: e.tensor_copy(out=BTMP[0:1, 1, :], in_=RBflat[0:1, :]),
          reads=[("c", "rb")], writes=btk)
    P.add("dve", lambda e: e.tensor_tensor(out=LOB[0:1, :], in0=BTMP[0:1, 0, :], in1=BTMP[0:1, 1, :],
                                           op=ALU.subtract),
          reads=btk, writes=lbk)
    dma("sp", CONST_SEM + "7", RBflat[1:2, :], LOB[0:1, :], reads=lbk, writes=[("c", "rb")])

    for l in range(2):
        if l not in layers:
            continue
        for g in range(8):
            xi, xk = xin(g % 2)
            dma("sp", "xin%d" % (g % 2), xi[:, 0:128], a_w_sp_d[l, g, :, :], writes=xk)
            pst, psk = bank()
            P.add("pe", lambda e, pst=pst, xi=xi: e.transpose(out=pst[:, 0:128], in_=xi[:, 0:128],
                                                             identity=IDENT[:, :]),
                  reads=xk + [("c", "ident")], writes=[psk])
            P.add("dve", lambda e, pst=pst, l=l, g=g: e.tensor_tensor(
                out=WST[:, l, g, :], in0=pst[:, 0:128], in1=UTRI[:, :], op=ALU.mult),
                reads=[psk, ("c", "utri")], writes=[("c", "wst")])

    pieces = []

    def kview(k):
        return lambda sl: RING[:, sl, :].rearrange("p (k n) -> p k n", k=k)

    for T in range(n_tiles):
        for l in layers:
            if l < 2:
                w = a_w_in_d[l].rearrange("(k p) n -> p k n", p=128)
                for ub in range(4):
                    pieces.append((("u", T, l, ub), [(kview(8), w[:, :, ub * 512:(ub + 1) * 512])]))
                for vb in range(4):
                    pieces.append((("v", T, l, vb), [(kview(8), w[:, :, 2048 + vb * 512:2048 + (vb + 1) * 512])]))
                wo = a_w_out_d[l].rearrange("(k p) n -> p k n", p=128)
                for ob in range(4):
                    pieces.append((("ao", T, l, ob), [(kview(16), wo[:, :, ob * 256:(ob + 1) * 256])]))
            else:
                j = l - 2
                if l == 2:
                    wk = w_k_d.rearrange("(k p) (kv d) -> p k kv d", p=128, d=64)
                    wv = w_v_d.rearrange("(k p) (kv d) -> p k kv d", p=128, d=64)
                    lst = []
                    for dup in range(2):
                        for kv in range(4):
                            lst.append((lambda sl, dup=dup, kv=kv: RING[:, sl, :].rearrange(
                                "p (k kv u d) -> p k kv u d", k=8, kv=4, u=2)[:, :, kv, dup, :],
                                wk[:, :, kv, :]))
                    pieces.append((("wk", T, l, 0), lst))
                    pieces.append((("wv", T, l, 0), [
                        (lambda sl: RING[:, sl, :].rearrange("p (k n) -> p k n", k=8)[:, :, 0:256],
                         w_v_d.rearrange("(k p) n -> p k n", p=128))]))
                wq = b_w_q_d[j].rearrange("(k p) n -> p k n", p=128)
                for qb in range(2):
                    pieces.append((("q", T, l, qb), [(kview(8), wq[:, :, qb * 512:(qb + 1) * 512])]))
                wo = b_w_o_d[j].rearrange("(k p) n -> p k n", p=128)
                for ob in range(2):
                    pieces.append((("bo", T, l, ob), [(kview(8), wo[:, :, ob * 512:(ob + 1) * 512])]))
            w1 = ffn_w1_d[l].rearrange("(k p) n -> p k n", p=128)
            for hf in range(2):
                for fb in range(4):
                    c0 = (hf * 4 + fb) * 512
                    pieces.append((("w1", T, l, hf * 4 + fb), [(kview(8), w1[:, :, c0:c0 + 512])]))
                w2 = ffn_w2_d[l][hf * 2048:(hf + 1) * 2048, :].rearrange("(k p) n -> p k n", p=128)
                for ob in range(4):
                    pieces.append((("w2", T, l, hf * 4 + ob), [(kview(16), w2[:, :, ob * 256:(ob + 1) * 256])]))

    st_w = {"next_load": 0, "next_use": 0}

    def acquire(n, tags):
        lo = st_w["next_use"]
        for i in range(n):
            assert pieces[lo + i][0][0] == tags[i][0] and pieces[lo + i][0][2:] == tags[i][1:], \
                (pieces[lo + i][0], tags[i])
        st_w["next_use"] = lo + n
        while st_w["next_load"] < min(lo + NSLOT, len(pieces)):
            pi = st_w["next_load"]
            sl = pi % NSLOT
            for (vf, src) in pieces[pi][1]:
                dma("pool", "slot%d" % sl, vf(sl), src, writes=[("slot", sl)])
            st_w["next_load"] = pi + 1
        return [((lo + i) % NSLOT) for i in range(n)]

    def mm_group(out_ap, pairs, reads, pkey):
        def fn(pe):
            n = len(pairs)
            ins = None
            for i, (l_, r_) in enumerate(pairs):
                ins = pe.matmul(out_ap, l_, r_, start=(i == 0), stop=(i == n - 1))
            return ins
        P.add("pe", fn, reads=reads, writes=[pkey])
        P.tick()

    def sub(s):
        return slice(s * 512, (s + 1) * 512)

    norm_state = {"done": False}

    def norm_A(s):
        sqv, sqk = sq_v()
        P.add("act", lambda e, s=s: e.activation(out=sqv, in_=XT[:, :, sub(s)], func=AF.Square),
              reads=[("X", c, s) for c in range(8)], writes=sqk)

    def norm_B(s, gcols, outs, exact=False):
        sqv, sqk = sq_v()
        pst, psk = bank()
        P.add("pe", lambda pe, pst=pst: [pe.matmul(pst[:, :], ONESK[:, :], sqv[:, k, :], start=(k == 0), stop=(k == 7))
                                         for k in range(8)][-1],
              reads=sqk + [("c", "onesk")], writes=[psk])
        rs, rk = rstd_v(s % 2)
        if exact or not OPT_LNEXP:
            P.add("act", lambda e, pst=pst, rs=rs: e.activation(
                out=rs, in_=pst[:, :], func=AF.Sqrt, bias=EPS, scale=1.0), reads=[psk], writes=rk)
            P.add("dve", lambda e, rs=rs: e.reciprocal(out=rs, in_=rs), reads=rk, writes=rk)
        else:
            P.add("act", lambda e, pst=pst, rs=rs: e.activation(
                out=rs, in_=pst[:, :], func=AF.Ln, bias=EPS, scale=1.0), reads=[psk], writes=rk)
            P.add("act", lambda e, rs=rs: e.activation(out=rs, in_=rs, func=AF.Exp, scale=-0.5),
                  reads=rk, writes=rk)
        cnt = 0
        for gi, gc in enumerate(gcols):
            tens, coff, kn = outs[gi]
            for k in range(8):
                if exact or k < 5 or not OPT_HNSPLIT:
                    P.add("dve", lambda e, k=k, s=s, rs=rs, gc=gc, tens=tens, coff=coff:
                          e.scalar_tensor_tensor(out=tens[:, coff + k, sub(s)], in0=XT[:, k, sub(s)],
                                                 scalar=GT[:, gc + k:gc + k + 1], in1=rs,
                                                 op0=ALU.mult, op1=ALU.mult),
                          reads=[("X", k, s), ("c", "gt")] + rk, writes=[(kn, coff + k, s)])
                else:
                    tm, tmk = sview(5120 + (cnt % 2) * 512, 5120 + (cnt % 2 + 1) * 512)
                    cnt += 1
                    P.add("act", lambda e, k=k, s=s, gc=gc, tm=tm: e.activation(
                        out=tm, in_=XT[:, k, sub(s)], func=AF.Identity, scale=GT[:, gc + k:gc + k + 1]),
                        reads=[("X", k, s), ("c", "gt")], writes=tmk)
                    P.add("pool", lambda e, k=k, s=s, rs=rs, tm=tm, tens=tens, coff=coff: e.tensor_tensor(
                        out=tens[:, coff + k, sub(s)], in0=tm, in1=rs, op=ALU.mult),
                        reads=tmk + rk, writes=[(kn, coff + k, s)])

    def rms_norm(gcols, outs):
        if norm_state["done"]:
            norm_state["done"] = False
            return
        for s in range(NSUB):
            norm_A(s)
            norm_B(s, gcols, outs, exact=(outs[0][0] is XT))

    def norm_spec_mix(l):
        if l == 2:
            return [G_KV, G_MIX + l * 8], [(BIG, 8, "big"), (HN, 0, "hn")]
        return [G_MIX + l * 8], [(HN, 0, "hn")]

    def norm_spec_ffn(l):
        return [G_FFN + l * 8], [(HN, 0, "hn")]

    def proj_fm(tagname, T, l, nblk, blk0, in_tens, in_off, in_key, nk, evac):
        ncol = 4096 // nk // 128

        def groups(sl, b, s):
            wv_ = RING[:, sl, :].rearrange("p (k n) -> p k n", k=nk)
            for c4 in range(ncol):
                pst, psk = bank()
                mm_group(pst[:, :],
                         [(wv_[:, k, c4 * 128:(c4 + 1) * 128], in_tens[:, in_off + k, sub(s)])
                          for k in range(nk)],
                         reads=[("slot", sl)] + [(in_key, in_off + k, s) for k in range(nk)], pkey=psk)
                evac(pst, psk, (blk0 + b) * ncol + c4, s)

        sls_ = acquire(2, [(tagname, l, blk0), (tagname, l, blk0 + 1)])
        for s in range(NSUB):
            for bi in range(2):
                groups(sls_[bi], bi, s)
        for b in range(2, nblk):
            (sl,) = acquire(1, [(tagname, l, blk0 + b)])
            for s in range(NSUB):
                groups(sl, b, s)

    def proj_fm_souter(tagname, T, l, nblk, blk0, in_tens, in_off, in_key, nk, evac, next_norm=None,
                       pre_hook=None):
        sls_ = acquire(nblk, [(tagname, l, blk0 + b) for b in range(nblk)])
        ncol = 4096 // nk // 128
        ngrp = [0]
        for s in range(NSUB):
            for b in range(nblk):
                wv_ = RING[:, sls_[b], :].rearrange("p (k n) -> p k n", k=nk)
                for c4 in range(ncol):
                    pst, psk = bank()
                    mm_group(pst[:, :],
                             [(wv_[:, k, c4 * 128:(c4 + 1) * 128], in_tens[:, in_off + k, sub(s)])
                              for k in range(nk)],
                             reads=[("slot", sls_[b])] + [(in_key, in_off + k, s) for k in range(nk)], pkey=psk)
                    evac(pst, psk, (blk0 + b) * ncol + c4, s)
                    if next_norm is not None:
                        cq = (blk0 + b) * ncol + c4 - (blk0 * ncol)
                        sqv_, _ = sq_v()
                        _, sqkc = sview(2048 + cq * 256, 2048 + (cq + 1) * 256)
                        P.add("act", lambda e, cq=cq, s=s, sqv_=sqv_: e.activation(
                            out=sqv_[:, cq, :], in_=XT[:, cq, sub(s)], func=AF.Square),
                            reads=[("X", cq, s)], writes=sqkc)
                    ngrp[0] += 1
                    if pre_hook is not None and ngrp[0] == pre_hook[0]:
                        pre_hook[1]()
            if next_norm is not None:
                gcols, outs = next_norm
                P.defer(1,
                        lambda s=s, gcols=gcols, outs=outs: norm_B(s, gcols, outs, exact=(outs[0][0] is XT)))
        if next_norm is not None:
            norm_state["done"] = True

    def evac_resid(pst, psk, c, s):
        P.add("dve", lambda e: e.tensor_tensor(out=XT[:, c, sub(s)], in0=XT[:, c, sub(s)],
                                               in1=pst[:, :], op=ALU.add),
              reads=[psk, ("X", c, s)], writes=[("X", c, s)])

    def ffn(T, l, nxt=None):
        rms_norm([G_FFN + l * 8], [(HN, 0, "hn")])
        for hf in range(2):
            def evac_relu2(pst, psk, c, s, hf=hf):
                cc = c - hf * 16
                rv, rk = relu_v((cc + s) % 2)
                P.add("act", lambda e: e.activation(out=rv, in_=pst[:, :], func=AF.Relu),
                      reads=[psk], writes=rk)
                P.add("pool", lambda e: e.tensor_tensor(out=BIG[:, cc, sub(s)], in0=rv, in1=rv, op=ALU.mult),
                      reads=rk, writes=[("big", cc, s)])
            proj_fm("w1", T, l, 4, hf * 4, HN, 0, "hn", 8, evac_relu2)

            def evac_r2(pst, psk, c, s, hf=hf):
                evac_resid(pst, psk, c - hf * 8, s)
            if hf == 1:
                proj_fm_souter("w2", T, l, 4, hf * 4, BIG, 0, "big", 16, evac_r2, next_norm=nxt)
            else:
                proj_fm("w2", T, l, 4, hf * 4, BIG, 0, "big", 16, evac_r2)

    def mixer_a(T, l):
        rms_norm([G_MIX + l * 8], [(HN, 0, "hn")])
        dma("sp", "lng", LNG, a_ln_g_d[l:l + 1, :].partition_broadcast(128)[:, 0],
            writes=[("KT", 0), ("KT", 1)])

        def evac_gelu(pst, psk, c, s):
            P.add("act", lambda e: e.activation(out=BIG[:, c, sub(s)], in_=pst[:, :], func=AF.Gelu),
                  reads=[psk], writes=[("big", c, s)])
        proj_fm("u", T, l, 4, 0, HN, 0, "hn", 8, evac_gelu)

        sls = acquire(4, [("v", l, vb) for vb in range(4)])

        def v_stage(t):
            s = t // 4
            tsl = slice(t * 128, (t + 1) * 128)
            vt, vk = v_t(t % 2)
            vn, vnk = vn_t(t % 2)
            skey = ("stat", t % 2)
            mvk = ("mv", t % 2)
            for vb in range(4):
                wv_ = RING[:, sls[vb], :].rearrange("p (k n) -> p k n", k=8)
                pst, psk = bank()
                mm_group(pst[:, :], [(HN[:, k, tsl], wv_[:, k, :]) for k in range(8)],
                         reads=[("slot", sls[vb])] + [("hn", k, s) for k in range(8)], pkey=psk)
                P.add("act", lambda e, pst=pst, vt=vt, vb=vb: e.activation(
                    out=vt[:, vb * 512:(vb + 1) * 512], in_=pst[:, :], func=AF.Gelu),
                    reads=[psk], writes=vk[vb * 2:(vb + 1) * 2])
                P.add("dve", lambda e, vt=vt, vb=vb, t=t: e.bn_stats(
                    out=STAT[:, t % 2, vb, :], in_=vt[:, vb * 512:(vb + 1) * 512]),
                    reads=vk[vb * 2:(vb + 1) * 2], writes=[skey])
            P.add("dve", lambda e, t=t: e.bn_aggr(out=MV[:, t % 2, 0:2],
                                                  in_=STAT[:, t % 2, :, :].rearrange("p a b -> p (a b)")),
                  reads=[skey], writes=[mvk])
            P.add("act", lambda e, t=t: e.activation(out=MV[:, t % 2, 1:2], in_=MV[:, t % 2, 1:2],
                                                     func=AF.Sqrt, bias=EPS, scale=1.0),
                  reads=[mvk], writes=[mvk])
            P.add("dve", lambda e, t=t: e.reciprocal(out=MV[:, t % 2, 1:2], in_=MV[:, t % 2, 1:2]),
                  reads=[mvk], writes=[mvk])
            P.add("dve", lambda e, t=t: e.scalar_tensor_tensor(
                out=MV[:, t % 2, 2:3], in0=MV[:, t % 2, 0:1], scalar=-1.0, in1=MV[:, t % 2, 1:2],
                op0=ALU.mult, op1=ALU.mult), reads=[mvk], writes=[mvk])
            for half, eng in ((0, "pool"), (1, "dve" if OPT_LNHALF else "pool")):
                hs = slice(half * 1024, (half + 1) * 1024)
                vkh = vk[half * 4:(half + 1) * 4]
                vnkh = vnk[half * 2:(half + 1) * 2]
                P.add("act", lambda e, t=t, vt=vt, hs=hs: e.activation(
                    out=vt[:, hs], in_=vt[:, hs], func=AF.Identity, bias=MV[:, t % 2, 2:3], scale=MV[:, t % 2, 1:2]),
                    reads=[mvk] + vkh, writes=vkh)
                P.add(eng, lambda e, vt=vt, vn=vn, hs=hs: e.tensor_tensor(out=vn[:, hs], in0=vt[:, hs], in1=LNG[:, hs],
                                                                         op=ALU.mult),
                      reads=vkh + [("KT", 0), ("KT", 1)], writes=vnkh)

        def sp_stage(t):
            s = t // 4
            tsl = slice(t * 128, (t + 1) * 128)
            vn, vnk = vn_t(t % 2)
            for c0 in range(0, 16, 4):
                pst, psk = bank()

                def fn(pe, pst=pst, c0=c0, vn=vn):
                    ins = None
                    for ci in range(4):
                        c = c0 + ci
                        g = c // 2
                        o_ = pst[:, ci * 128:(ci + 1) * 128]
                        pe.matmul(o_, vn[:, c * 128:(c + 1) * 128], WST[:, l, g, :], start=True, stop=False)
                        ins = pe.matmul(o_, E2[:, :], RB[:, l, g, :], start=False, stop=True)
                    return ins
                P.add("pe", fn, reads=vnk + [("c", "wst"), ("c", "rb"), ("c", "e2")], writes=[psk])
                P.add("dve", lambda e, pst=pst, c0=c0, tsl=tsl: e.tensor_tensor(
                    out=BIG[:, c0:c0 + 4, tsl], in0=pst[:, :].rearrange("p (a b) -> p a b", a=4),
                    in1=BIG[:, c0:c0 + 4, tsl], op=ALU.mult),
                    reads=[psk] + [("big", c0 + ci, s) for ci in range(4)],
                    writes=[("big", c0 + ci, s) for ci in range(4)])

        v_stage(0)
        for t in range(NCH - 1):
            v_stage(t + 1)
            sp_stage(t)
        proj_fm_souter("ao", T, l, 4, 0, BIG, 0, "big", 16, evac_resid, next_norm=norm_spec_ffn(l),
                       pre_hook=(4, lambda: sp_stage(NCH - 1)))

    def mixer_b(T, l):
        j = l - 2
        if l == 2:
            rms_norm(*norm_spec_mix(2))
            (slk,) = acquire(1, [("wk", l, 0)])
            wkv = RING[:, slk, :].rearrange("p (k n) -> p k n", k=8)
            for s in range(NSUB):
                for kv in range(4):
                    pst, psk = bank()
                    mm_group(pst[:, :], [(wkv[:, k, kv * 128:(kv + 1) * 128], BIG[:, 8 + k, sub(s)])
                                         for k in range(8)],
                             reads=[("slot", slk)] + [("big", 8 + k, s) for k in range(8)], pkey=psk)
                    P.add("act", lambda e, pst=pst, kv=kv, s=s: e.activation(
                        out=KTC[:, kv, sub(s)], in_=pst[:, :], func=AF.Copy),
                        reads=[psk], writes=[("KT", s)])
            (slv,) = acquire(1, [("wv", l, 0)])
            wvv = RING[:, slv, :].rearrange("p (k n) -> p k n", k=8)
            for t in range(NCH):
                s = t // 4
                pst, psk = bank()
                mm_group(pst[:, 0:256], [(BIG[:, 8 + k, t * 128:(t + 1) * 128], wvv[:, k, 0:256]) for k in range(8)],
                         reads=[("slot", slv)] + [("big", 8 + k, s) for k in range(8)], pkey=psk)
                P.add("act", lambda e, pst=pst, t=t: e.activation(
                    out=VC[:, t, :].rearrange("p (g c) -> p g c", g=4)[:, :, 64:128],
                    in_=pst[:, 0:256].rearrange("p (g c) -> p g c", g=4), func=AF.Copy),
                    reads=[psk], writes=[("V", s)])
        else:
            rms_norm([G_MIX + l * 8], [(HN, 0, "hn")])

        def evac_q(pst, psk, c, s):
            P.add("act", lambda e: e.mul(out=BIG[:, c, sub(s)], in_=pst[:, :], mul=0.125),
                  reads=[psk], writes=[("big", c, s)])
        proj_fm("q", T, l, 2, 0, HN, 0, "hn", 8, evac_q)

        its = [(t, kvg) for t in range(NCH) for kvg in range(4)]

        def jcs_of(t):
            return [1] if (T * NCH + t) == 0 else [0, 1]

        def e32_rng(n, par, lo):
            base = (0, 3072, 4096)[n % 3] + par * 512
            return sview(base + lo, base + 512)

        def eb_rng(n, par, lo):
            base = (1024, 1536, 2560)[n % 3] + par * 256
            a, k = sview(base, base + 256)
            return a.bitcast(BF16), k

        def stage_qk(n):
            t, kvg = its[n]
            s = t // 4
            tsl = slice(t * 128, (t + 1) * 128)
            jcs = jcs_of(t)
            lo = 0 if 0 in jcs else 256
            for par in range(2):
                psl = slice(par * 64, (par + 1) * 64)
                pst, psk = bank()

                def fn(pe, pst=pst, psl=psl, kvg=kvg, t=t, tsl=tsl, jcs=jcs):
                    ins = None
                    for jc in jcs:
                        if jc == 1:
                            kt = KTC[psl, kvg, tsl]
                        elif t == 0:
                            kt = KTP[psl, kvg, :]
                        else:
                            kt = KTC[psl, kvg, (t - 1) * 128:t * 128]
                        o_ = pst[:, jc * 256:(jc + 1) * 256].rearrange("p (d e) -> p d e", d=2)
                        ins = pe.matmul(o_, kt, BIG[psl, 2 * kvg:2 * kvg + 2, tsl], start=True, stop=True)
                    return ins
                rds = [("big", 2 * kvg, s), ("big", 2 * kvg + 1, s), ("KT", s)]
                if 0 in jcs:
                    rds.append(("KTp",) if t == 0 else ("KT", (t - 1) // 4))
                P.add("pe", fn, reads=rds, writes=[psk])
                ea, ek = e32_rng(n, par, lo)
                P.add("act", lambda e, pst=pst, lo=lo, ea=ea: e.activation(
                    out=ea, in_=pst[:, lo:512], func=AF.Exp), reads=[psk], writes=ek)

        def stage_mul(n):
            t, kvg = its[n]
            lo = 0 if 0 in jcs_of(t) else 256
            for par, eng in ((0, "pool"), (1, "pool")):
                ea, ek = e32_rng(n, par, lo)
                ba, bk = eb_rng(n, par, 0)
                P.add(eng, lambda e, ea=ea, ba=ba, kvg=kvg, par=par, lo=lo: e.tensor_tensor(
                    out=ba[:, lo:512], in0=ea, in1=EXPB[:, kvg * 2 + par, lo:512], op=ALU.mult),
                    reads=ek + [("c", "expb")], writes=bk)

        def stage_pv(n):
            t, kvg = its[n]
            s = t // 4
            tsl = slice(t * 128, (t + 1) * 128)
            jcs = jcs_of(t)
            ebs = [eb_rng(n, par, 0) for par in range(2)]
            NDb, NDk = bank()

            def fnpv(pe, kvg=kvg, t=t, ebs=ebs, NDb=NDb, jcs=jcs):
                ins = None
                for use_v in (True, False):
                    o_ = NDb[:, (0 if use_v else 256):(256 if use_v else 512)].rearrange("p (d e) -> p d e", d=2)
                    mms = [(par, jc) for par in range(2) for jc in jcs]
                    for mi, (par, jc) in enumerate(mms):
                        ev = ebs[par][0].rearrange("p (c d e) -> p c d e", c=2, d=2)
                        c0 = 64 if par == 0 else 0
                        if not use_v:
                            lhs = HONES[:, c0:c0 + 128]
                        elif jc == 1:
                            lhs = VC[:, t, kvg * 192 + c0:kvg * 192 + c0 + 128]
                        elif t == 0:
                            lhs = VP[:, kvg * 192 + c0:kvg * 192 + c0 + 128]
                        else:
                            lhs = VC[:, t - 1, kvg * 192 + c0:kvg * 192 + c0 + 128]
                        ins = pe.matmul(o_, lhs, ev[:, jc, :, :], start=(mi == 0), stop=(mi == len(mms) - 1))
                return ins
            rds = ebs[0][1] + ebs[1][1] + [("V", s), ("c", "hones")]
            if 0 in jcs:
                rds.append(("Vp",) if t == 0 else ("V", (t - 1) // 4))
            P.add("pe", fnpv, reads=rds, writes=[NDk])
            rd, rdk = rden_v(n % 2)
            P.add("dve", lambda e, NDb=NDb, rd=rd, kvg=kvg: e.tensor_scalar(
                out=rd[:, 0:128], in0=NDb[:, 256:384], scalar1=ESC[:, j, kvg, 0:1], scalar2=None, op0=ALU.add),
                reads=[NDk, ("c", "esc")], writes=[rdk[0], ("rdh", n % 2, 0)])
            P.add("dve", lambda e, rd=rd: e.reciprocal(out=rd[:, 0:128], in_=rd[:, 0:128]),
                  reads=[("rdh", n % 2, 0)], writes=[("rdh", n % 2, 0)])
            P.add("act", lambda e, NDb=NDb, rd=rd, kvg=kvg: e.activation(
                out=rd[:, 128:256], in_=NDb[:, 384:512], func=AF.Ln, bias=ESC[:, j, kvg, 1:2], scale=1.0),
                reads=[NDk, ("c", "esc")], writes=[rdk[0], ("rdh", n % 2, 1)])
            P.add("act", lambda e, rd=rd: e.activation(out=rd[:, 128:256], in_=rd[:, 128:256], func=AF.Exp, scale=-1.0),
                  reads=[("rdh", n % 2, 1)], writes=[("rdh", n % 2, 1)])
            P.add("dve", lambda e, NDb=NDb, rd=rd, kvg=kvg, tsl=tsl: e.tensor_tensor(
                out=BIG[:, 8 + 2 * kvg:8 + 2 * kvg + 2, tsl],
                in0=NDb[:, 0:256].rearrange("p (d e) -> p d e", d=2),
                in1=rd.rearrange("p (d e) -> p d e", d=2), op=ALU.mult),
                reads=[NDk, ("rdh", n % 2, 0), ("rdh", n % 2, 1)],
                writes=[("big", 8 + 2 * kvg, s), ("big", 8 + 2 * kvg + 1, s)])

        for n0 in range(OPT_DEPTH):
            stage_qk(n0)
        for n in range(len(its)):
            if n + OPT_DEPTH < len(its):
                stage_qk(n + OPT_DEPTH)
            stage_mul(n)
            stage_pv(n)
        if debug == "attn":
            dbg_d = nc.dram_tensor("dbg", [128, 16 * TT], BF16, kind="ExternalOutput").ap()
            dma("sp", "dbg", dbg_d[:, :], BIG[:, :, :].rearrange("p a b -> p (a b)"),
                reads=[("big", c, s) for c in range(16) for s in range(NSUB)])
            raise StopIteration
        proj_fm_souter("bo", T, l, 2, 0, BIG, 8, "big", 8, evac_resid, next_norm=norm_spec_ffn(l))
        if l == 3 or (3 not in layers):
            P.add("pool", lambda e: e.tensor_copy(out=KTP[:, :, :], in_=KTC[:, :, TT - 128:TT]),
                  reads=[("KT", 1)], writes=[("KTp",)])
            P.add("pool", lambda e: e.tensor_copy(out=VP[:, :], in_=VC[:, NCH - 1, :]),
                  reads=[("V", 1)], writes=[("Vp",)])

    for T in range(n_tiles):
        tok0 = T * TT
        for t in range(NCH):
            xi, xk = xin(t % 4)
            dma("sp", "xin%d" % (t % 4), xi, x_d[tok0 + t * 128:tok0 + (t + 1) * 128, :], writes=xk)
            for cg in range(2):
                pst, psk = bank()

                def fn(pe, pst=pst, xi=xi, cg=cg):
                    ins = None
                    for ci in range(4):
                        c = cg * 4 + ci
                        ins = pe.transpose(out=pst[:, ci * 128:(ci + 1) * 128], in_=xi[:, c * 128:(c + 1) * 128],
                                           identity=IDENT[:, :])
                    return ins
                P.add("pe", fn, reads=xk + [("c", "ident")], writes=[psk])
                P.add("act", lambda e, pst=pst, cg=cg, t=t: e.activation(
                    out=XT[:, cg * 4:(cg + 1) * 4, t * 128:(t + 1) * 128],
                    in_=pst[:, :].rearrange("p (a b) -> p a b", a=4), func=AF.Copy),
                    reads=[psk], writes=[("X", cg * 4 + ci, t // 4) for ci in range(4)])
        try:
            for l in layers:
                if l < 2:
                    mixer_a(T, l)
                else:
                    mixer_b(T, l)
                if debug == "noffn":
                    for _ in range(16):
                        acquire(1, [pieces[st_w["next_use"]][0][0:1] + pieces[st_w["next_use"]][0][2:]])
                    continue
                li = layers.index(l)
                if li + 1 < len(layers):
                    nxt = norm_spec_mix(layers[li + 1])
                elif final_norm:
                    nxt = ([G_FIN], [(XT, 0, "X")])
                else:
                    nxt = None
                ffn(T, l, nxt)
        except StopIteration:
            st_w["next_use"] = len(pieces)
            break
        if final_norm:
            rms_norm([G_FIN], [(XT, 0, "X")])
        P.flush()
        for t in range(NCH):
            yo, yk = xin(t % 4)
            for cg in range(2):
                pst, psk = bank()

                def fn(pe, pst=pst, cg=cg, t=t):
                    ins = None
                    for ci in range(4):
                        c = cg * 4 + ci
                        ins = pe.transpose(out=pst[:, ci * 128:(ci + 1) * 128],
                                           in_=XT[:, c, t * 128:(t + 1) * 128], identity=IDENT[:, :])
                    return ins
                P.add("pe", fn, reads=[("X", cg * 4 + ci, t // 4) for ci in range(4)] + [("c", "ident")],
                      writes=[psk])
                P.add("act", lambda e, pst=pst, cg=cg, yo=yo: e.activation(
                    out=yo[:, cg * 512:(cg + 1) * 512], in_=pst[:, :], func=AF.Copy),
                    reads=[psk], writes=yk)
            dma("sp", "xin%d" % (t % 4), y_d[tok0 + t * 128:tok0 + (t + 1) * 128, :], yo, reads=yk)

    assert st_w["next_use"] == len(pieces), (st_w, len(pieces))

    P.finalize()
    keys = list(P.ENG) + sorted(P.dma_cnt.keys())
    with ExitStack() as es:
        sems = {k: es.enter_context(nc.semaphore("s_" + k)) for k in keys}
        block = es.enter_context(nc.Block())

        @block.tensor
        def _(e):
            P.emit("pe", e, sems)

        @block.scalar
        def _(e):
            P.emit("act", e, sems)

        @block.vector
        def _(e):
            P.emit("dve", e, sems)

        @block.gpsimd
        def _(e):
            P.emit("pool", e, sems)

        @block.sync
        def _(e):
            P.emit("sp", e, sems, final_waits=("xin0", "xin1", "xin2", "xin3", "dbg"))
    return nc


def host_consts(mix_norm_g, ffn_norm_g, kv_norm_g, final_norm_g, rel_bias):
    f = np.float32
    gstack = np.concatenate([np.asarray(mix_norm_g, f).reshape(32, 128),
                             np.asarray(ffn_norm_g, f).reshape(32, 128),
                             np.asarray(kv_norm_g, f).reshape(8, 128),
                             np.asarray(final_norm_g, f).reshape(8, 128)], axis=0)
    bucket, inwin = _bucket_table()
    rb = np.asarray(rel_bias, f)
    bias_full = np.transpose(rb[bucket], (2, 0, 1))
    bf = bias_full.reshape(4, 2, 2, 128, 2, 128)
    biasg = np.ascontiguousarray(np.transpose(bf, (5, 0, 2, 4, 1, 3))).reshape(128, 16 * 256)
    amask = np.ascontiguousarray(
        np.transpose(inwin.reshape(128, 2, 128), (2, 1, 0))).astype(f).reshape(128, 256)
    ident = np.eye(128, dtype=f)
    utri = np.triu(np.ones((128, 128), f))
    return dict(gstack=np.ascontiguousarray(gstack), biasg=biasg, amask=amask, ident=ident, utri=utri)


_NC_CACHE = {}


def kernel(x, mix_norm_g, ffn_norm_g, a_w_in, a_ln_g, a_w_spatial, a_b_spatial, a_w_out,
           kv_norm_g, w_k, w_v, b_w_q, b_sinks, b_w_o, rel_bias, ffn_w1, ffn_w2, final_norm_g):
    f = np.float32
    x = np.asarray(x, f)
    B = x.shape[0]
    if "nc" not in _NC_CACHE:
        _NC_CACHE["nc"] = build_nc()
    nc = _NC_CACHE["nc"]
    common = host_consts(mix_norm_g, ffn_norm_g, kv_norm_g, final_norm_g, rel_bias)
    for k, v in dict(a_w_in=a_w_in, a_ln_g=a_ln_g, a_w_spatial=a_w_spatial, a_b_spatial=a_b_spatial,
                     a_w_out=a_w_out, w_k=w_k, w_v=w_v, b_w_q=b_w_q, b_sinks=b_sinks, b_w_o=b_w_o,
                     ffn_w1=ffn_w1, ffn_w2=ffn_w2).items():
        common[k] = np.ascontiguousarray(np.asarray(v, f))
    in_maps = []
    for b in range(B):
        m = dict(common)
        m["x"] = np.ascontiguousarray(x[b])
        in_maps.append(m)
    res = run_bass_kernel_spmd(nc, in_maps, core_ids=list(range(B)))
    return np.stack([np.asarray(r["y"], f) for r in res.results], axis=0)
```
